# Optimizing a Trainium2 kernel written in Bass

```python
import jax, jax.numpy as jnp
from jax import lax
import numpy as np

D_MODEL = 1024
BATCH = 4
SEQ = 4096
DEPTH = 4

CHUNK = 64
N_MIXERS = 4
MIX_WIDTH = D_MODEL
GROUP_WIDTH = MIX_WIDTH // N_MIXERS
N_HEADS_PER_MIXER = 4
HEAD_DIM = GROUP_WIDTH // N_HEADS_PER_MIXER
CONFORMER_KERNEL = 31
SHORT_CONV_KERNEL = 3
POOL_WINDOWS = (2, 4, 8, 16)
SGU_BLOCK = 128
N_IN_SLICES = 12
IN_WIDTH = N_IN_SLICES * GROUP_WIDTH
LN_EPS = 1e-5

kernel_name = "hybrid_conv_pool_sgu_deepnorm_trunk"


def layer_norm(x, g, b):
    xf = x.astype(jnp.float32)
    mu = jnp.mean(xf, axis=-1, keepdims=True)
    var = jnp.mean(jnp.square(xf - mu), axis=-1, keepdims=True)
    y = (xf - mu) * lax.rsqrt(var + LN_EPS)
    return (y * g + b).astype(x.dtype)


def causal_dwconv(x, w):
    k, c = w.shape
    return lax.conv_general_dilated(
        x, w[:, None, :].astype(x.dtype), window_strides=(1,), padding=[(k - 1, 0)],
        dimension_numbers=("NWC", "WIO", "NWC"), feature_group_count=c)


def multi_scale_pool(h):
    bsz, s, _ = h.shape
    hg = h.reshape(bsz, s, len(POOL_WINDOWS), HEAD_DIM).astype(jnp.float32)
    cs = jnp.cumsum(hg, axis=1)
    pos1 = jnp.arange(1, s + 1)
    means = []
    for g, w in enumerate(POOL_WINDOWS):
        c = cs[:, :, g]
        prev = jnp.pad(c[:, : s - w], ((0, 0), (w, 0), (0, 0)))
        cnt = jnp.minimum(pos1, w).astype(jnp.float32)[None, :, None]
        means.append((c - prev) / cnt)
    mean = jnp.stack(means, axis=2)
    return (mean - hg).astype(h.dtype)


def sgu_mask():
    idx = jnp.arange(SGU_BLOCK) // CHUNK
    return (idx[None, :] <= idx[:, None])


def hybrid_layer(x, ln_g, ln_b, w_in, b_in, conv_a_w, conv_a_b, norm_a_g, norm_a_b,
                 conv_b_w, pool_w, pool_scale, sgu_ln_g, sgu_ln_b, sgu_w, sgu_bias,
                 w_out, b_out):
    bsz, s, _ = x.shape
    alpha = float((2.0 * DEPTH) ** 0.25)
    h = jnp.einsum("bsd,de->bse", x, w_in) + b_in
    (a_val, a_glu, a_z, b_b, b_c, b_h, b_z, c_h, c_z, d_u, d_v, d_z) = jnp.split(h, N_IN_SLICES, axis=-1)

    a = a_val * jax.nn.sigmoid(a_glu)
    a = causal_dwconv(a, conv_a_w) + conv_a_b
    a = layer_norm(a.reshape(bsz, s, N_HEADS_PER_MIXER, HEAD_DIM),
                   norm_a_g.reshape(N_HEADS_PER_MIXER, HEAD_DIM),
                   norm_a_b.reshape(N_HEADS_PER_MIXER, HEAD_DIM)).reshape(bsz, s, GROUP_WIDTH)
    y_a = jax.nn.silu(a) * jax.nn.silu(a_z)

    y_b = b_b * causal_dwconv(b_c * b_h, conv_b_w) * jax.nn.silu(b_z)

    pooled = multi_scale_pool(c_h)
    y_c = jnp.einsum("bsgc,gcd->bsgd", pooled, pool_w).reshape(bsz, s, GROUP_WIDTH)
    y_c = y_c * pool_scale * jax.nn.silu(c_z)

    v = layer_norm(d_v, sgu_ln_g, sgu_ln_b)
    vb = v.reshape(bsz, s // SGU_BLOCK, SGU_BLOCK, N_HEADS_PER_MIXER, HEAD_DIM)
    w_s = jnp.where(sgu_mask()[None], sgu_w, jnp.zeros_like(sgu_w))
    sp = jnp.einsum("hij,bnjhc->bnihc", w_s, vb) + sgu_bias.T[None, None, :, :, None]
    y_d = d_u * sp.reshape(bsz, s, GROUP_WIDTH) * jax.nn.silu(d_z)

    mix = jnp.concatenate([y_a, y_b, y_c, y_d], axis=-1)
    out = jnp.einsum("bse,ed->bsd", mix, w_out) + b_out
    return layer_norm(alpha * x + out, ln_g, ln_b)


def setup_inputs(seed: int = 0) -> dict:
    key = jax.random.key(seed)
    ks = jax.random.split(key, 20)
    f32 = jnp.float32
    L = DEPTH
    beta = (8.0 * DEPTH) ** -0.25
    nrm = lambda k, shape, sc: jax.random.normal(k, shape, f32) * sc
    return {
        "x": jax.random.normal(ks[0], (BATCH, SEQ, D_MODEL), f32),
        "ln_g": 1.0 + nrm(ks[1], (L, D_MODEL), 0.05),
        "ln_b": nrm(ks[2], (L, D_MODEL), 0.02),
        "w_in": nrm(ks[3], (L, D_MODEL, IN_WIDTH), D_MODEL ** -0.5),
        "b_in": nrm(ks[4], (L, IN_WIDTH), 0.02),
        "conv_a_w": nrm(ks[5], (L, CONFORMER_KERNEL, GROUP_WIDTH), CONFORMER_KERNEL ** -0.5),
        "conv_a_b": nrm(ks[6], (L, GROUP_WIDTH), 0.02),
        "norm_a_g": 1.0 + nrm(ks[7], (L, GROUP_WIDTH), 0.05),
        "norm_a_b": nrm(ks[8], (L, GROUP_WIDTH), 0.02),
        "conv_b_w": nrm(ks[9], (L, SHORT_CONV_KERNEL, GROUP_WIDTH), SHORT_CONV_KERNEL ** -0.5),
        "pool_w": nrm(ks[10], (L, len(POOL_WINDOWS), HEAD_DIM, HEAD_DIM), HEAD_DIM ** -0.5),
        "pool_scale": 1.0 + nrm(ks[11], (L, GROUP_WIDTH), 0.1),
        "sgu_ln_g": 1.0 + nrm(ks[12], (L, GROUP_WIDTH), 0.05),
        "sgu_ln_b": nrm(ks[13], (L, GROUP_WIDTH), 0.02),
        "sgu_w": nrm(ks[14], (L, N_HEADS_PER_MIXER, SGU_BLOCK, SGU_BLOCK), SGU_BLOCK ** -0.5),
        "sgu_bias": 1.0 + nrm(ks[15], (L, N_HEADS_PER_MIXER, SGU_BLOCK), 0.1),
        "w_out": nrm(ks[16], (L, MIX_WIDTH, D_MODEL), beta * MIX_WIDTH ** -0.5),
        "b_out": nrm(ks[17], (L, D_MODEL), 0.01),
    }


def reference(x, ln_g, ln_b, w_in, b_in, conv_a_w, conv_a_b, norm_a_g, norm_a_b,
              conv_b_w, pool_w, pool_scale, sgu_ln_g, sgu_ln_b, sgu_w, sgu_bias,
              w_out, b_out):
    h = x
    for l in range(DEPTH):
        h = hybrid_layer(h, ln_g[l], ln_b[l], w_in[l], b_in[l], conv_a_w[l], conv_a_b[l],
                         norm_a_g[l], norm_a_b[l], conv_b_w[l], pool_w[l], pool_scale[l],
                         sgu_ln_g[l], sgu_ln_b[l], sgu_w[l], sgu_bias[l], w_out[l], b_out[l])
    return h
```

```python
import numpy as np
import concourse.bass as bass
import concourse.mybir as mybir
from concourse.bass_utils import run_bass_kernel_spmd

F32 = mybir.dt.float32
BF16 = mybir.dt.bfloat16
AF = mybir.ActivationFunctionType
ALU = mybir.AluOpType

D = 1024
NCH = 8
SEQ = 4096
BATCH = 4
DEPTH = 4
TMAIN = 2048
HALO = 256
TTOT = TMAIN + HALO
TW = 256
KA = 31
EPS = 1e-5
ALPHA = float((2.0 * DEPTH) ** 0.25)
REORDER = True
import os as _os8
APPLY_HI_ENG = _os8.environ.get("K_APPLY_HI", "dve")
import os as _os7
CAST_POOL = _os7.environ.get("K_CAST_POOL", "0") == "1"
import os as _os6
TBL_PEN = float(_os6.environ.get("K_TBL_PEN", "0.0"))
TBL_HOLD = float(_os6.environ.get("K_TBL_HOLD", "0.0"))
import os as _os4
OVH_DVE = float(_os4.environ.get("K_OVH_DVE", "0.1"))
OVH_ACT = float(_os4.environ.get("K_OVH_ACT", "0.08"))
import os as _os3
ZB_ACT = _os3.environ.get("K_ZB_ACT", "0") == "1"
XNEW_ACT = _os3.environ.get("K_XNEW_ACT", "1") == "1"
APPLY_DVE = tuple(int(c) for c in _os3.environ.get("K_APPLY_DVE", "0246"))
SYNC_ALL = True
NXBF = 3
TAGS = False
import os as _os2
PRIO_BL = _os2.environ.get("K_PRIO_BL", "0") == "1"
SLOT_ = float(_os2.environ.get("K_SLOT", "0.3"))
import os as _os
ZSQ_ACT = _os.environ.get("K_ZSQ_ACT", "0") == "1"
POOL_ENG = _os.environ.get("K_POOL_ENG", "pool")
YA_ENG = _os.environ.get("K_YA_ENG", "dve")
YD_ENG = _os.environ.get("K_YD_ENG", "dve")
LAG_ = int(_os.environ.get("K_LAG", "6"))
WINDOW_ = int(_os.environ.get("K_WINDOW", "96"))
NACTIVE_ = int(_os.environ.get("K_NACTIVE", "3"))

R_BIN = 0
R_CAW = 24
R_CAB = 86
R_NAG = 88
R_NAB = 90
R_CBW = 92
R_PSC = 98
R_BOUT = 100
R_LNG = 108
R_LNB = 116
R_PINV = 124
R_FLAG = 126
R_NEG1 = 127

(S_AVAL, S_AGLU, S_AZ, S_BB, S_BC, S_BH, S_BZ, S_CH, S_CZ, S_DU, S_DV, S_DZ) = range(12)


class Op:
    __slots__ = ("id", "eng", "fn", "waits", "is_dma", "dsem", "dval", "seq", "clock", "target", "val",
                 "preds", "cost", "lat", "fin", "start", "tag", "tbl")


OVH = {"dve": OVH_DVE, "act": OVH_ACT, "pool": 0.1}
DEF_COST = {"pe": 0.118, "act": 0.42, "dve": 0.3, "pool": 0.8, "sp": 0.1}


class Trk:
    ENGS = ("pe", "act", "dve", "pool", "sp")

    def __init__(self):
        self.ops = []
        self.eng_ops = {e: [] for e in self.ENGS}
        self.last_w = {}
        self.readers = {}
        self.dma_sems = {}
        self.n_dma_sem = 0

    def op(self, eng, fn, reads=(), writes=(), dma=None, cost=None, lat=0.0, tbl=None):
        o = Op()
        o.tbl = tbl
        o.id = len(self.ops)
        o.eng = eng
        o.fn = fn
        o.is_dma = dma is not None
        o.target = False
        o.val = None
        o.cost = (DEF_COST[eng] if cost is None else cost) + OVH.get(eng, 0.0)
        o.tag = None
        if TAGS:
            import sys as _s
            f = _s._getframe(1)
            while f.f_code.co_name in ("act", "tt", "stt", "<lambda>", "op"):
                f = f.f_back
            o.tag = "%s:%d" % (f.f_code.co_name, f.f_lineno)
        o.lat = lat
        if o.is_dma:
            if dma not in self.dma_sems:
                self.dma_sems[dma] = [self.n_dma_sem, 0]
                self.n_dma_sem += 1
            self.dma_sems[dma][1] += 16
            o.dsem = dma
            o.dval = self.dma_sems[dma][1]
            if cost is None:
                o.cost = 0.1 if eng == "sp" else 1.0
            if lat == 0.0:
                o.lat = 4.0
        preds = {}
        for r in reads:
            w = self.last_w.get(r)
            if w is not None:
                preds[w.id] = (w, True)
        for r in writes:
            w = self.last_w.get(r)
            if w is not None:
                ns = ((SYNC_ALL and eng != "pe") or w.is_dma or o.is_dma or w.eng != eng or eng == "pool")
                if w.id not in preds or ns:
                    preds[w.id] = (w, ns or preds.get(w.id, (w, False))[1])
            for rd in self.readers.get(r, ()):
                ns = ((SYNC_ALL and eng != "pe") or rd.is_dma or o.is_dma or rd.eng != eng or eng == "pool")
                if rd.id not in preds:
                    preds[rd.id] = (rd, ns)
                elif ns:
                    preds[rd.id] = (rd, True)
        preds.pop(o.id, None)
        o.preds = list(preds.values())
        self.eng_ops[eng].append(o)
        self.ops.append(o)
        for r in reads:
            self.readers.setdefault(r, []).append(o)
        for r in writes:
            self.last_w[r] = o
            self.readers[r] = []
        return o

    def merge(self, dst, srcs):
        lst = self.readers.setdefault(dst, [])
        for k in srcs:
            lst.extend(self.readers.get(k, ()))
            w = self.last_w.get(k)
            if w is not None:
                lst.append(w)

    def finalize(self, reorder=True, window=96):
        pend = {e: list(self.eng_ops[e]) for e in self.ENGS}
        bl = [0.0] * len(self.ops)
        for o in reversed(self.ops):
            v = bl[o.id] + o.cost + o.lat
            bl[o.id] = v
            for (p, _) in o.preds:
                if bl[p.id] < v:
                    bl[p.id] = v
        use_bl = PRIO_BL
        free = {e: 0.0 for e in self.ENGS}
        act_state = None
        n_switch = 0
        sched = []
        done = set()
        order = {e: [] for e in self.ENGS}
        total = len(self.ops)
        while len(sched) < total:
            best = None
            for e in self.ENGS:
                lst = pend[e]
                if not lst:
                    continue
                lim = window if reorder else 1
                cnt = 0
                for o in lst:
                    cnt += 1
                    if cnt > lim:
                        break
                    ok = True
                    t = free[e]
                    for (p, _) in o.preds:
                        if p.id not in done:
                            ok = False
                            break
                        if p.fin > t:
                            t = p.fin
                    if not ok:
                        continue
                    pen = 0.0
                    if e == "act" and o.tbl is not None and act_state is not None and o.tbl != act_state:
                        pen = TBL_PEN + (TBL_HOLD if act_state == "sqrt" else 0.0)
                    if use_bl:
                        key = (round(t / SLOT_) , -bl[o.id], o.id)
                    else:
                        key = (t + pen, o.id)
                    if best is None or key < best[0]:
                        best = (key, e, o, t)
                    if (not use_bl) and pen == 0.0 and t <= free[e]:
                        break
            assert best is not None, "scheduler deadlock"
            _, e, o, t = best
            o.start = t
            c_eff = o.cost
            if e == "act" and o.tbl is not None:
                if act_state is not None and o.tbl != act_state:
                    c_eff += 1.3
                    n_switch += 1
                act_state = o.tbl
            free[e] = t + c_eff
            o.fin = t + c_eff + o.lat
            pend[e].remove(o)
            done.add(o.id)
            sched.append(o)
            order[e].append(o)
        self.eng_ops = order
        self.makespan = max(o.fin for o in sched)
        self.n_switch = n_switch
        clock = {e: {} for e in self.ENGS}
        nseq = {e: 0 for e in self.ENGS}
        for o in sched:
            eng = o.eng
            clk = clock[eng]
            waits = []
            for (d, ns) in sorted(o.preds, key=lambda q: -(q[0].start + 1e-9 * q[0].id)):
                if not ns:
                    continue
                if d.is_dma:
                    known = clk.get(("d", d.id), 0) >= 1
                else:
                    known = clk.get(d.eng, 0) >= d.seq
                if known:
                    continue
                waits.append(d)
                d.target = True
                for k, v in d.clock.items():
                    if isinstance(k, tuple) and not (d.is_dma and k == ("d", d.id)):
                        continue
                    if clk.get(k, 0) < v:
                        clk[k] = v
            o.waits = waits
            nseq[eng] += 1
            if o.is_dma:
                o.seq = None
                o.clock = dict(clk)
                o.clock[("d", o.id)] = 1
            else:
                o.seq = nseq[eng]
                c2 = dict(clk)
                c2[eng] = o.seq
                o.clock = c2
        for e in self.ENGS:
            n = 0
            for o in self.eng_ops[e]:
                if (not o.is_dma) and o.target:
                    n += 1
                    o.val = n


def fsz(ap):
    n = 1
    for d in ap.shape[1:]:
        n *= d
    return n


def c_act(n):
    return (250 + n) / 1200.0


def c_dve(n):
    return (170 + n) / 960.0


def c_pool(n):
    return (200 + 2.9 * n) / 1200.0


class Pool_:
    def __init__(self, items):
        self.free = list(items)

    def get(self):
        assert self.free, "pool exhausted"
        return self.free.pop(0)

    def put(self, it):
        self.free.append(it)


def build(NL=DEPTH, n_tiles_dbg=None):
    nc = bass.Bass("TRN2", target_bir_lowering=False)
    x_d = nc.dram_tensor("x", [TTOT, D], F32, kind="ExternalInput").ap()
    win_d = nc.dram_tensor("w_in", [NL, D, 3072], F32, kind="ExternalInput").ap()
    wout_d = nc.dram_tensor("w_out", [NL, D, D], F32, kind="ExternalInput").ap()
    prm_d = nc.dram_tensor("prm", [NL, 128, 128], F32, kind="ExternalInput").ap()
    dvb_d = nc.dram_tensor("dvb", [NL, 3 * 256], F32, kind="ExternalInput").ap()
    sguw_d = nc.dram_tensor("sguwT", [NL, 4, 128, 128], F32, kind="ExternalInput").ap()
    sgub_d = nc.dram_tensor("sgub", [NL, 4, 128], F32, kind="ExternalInput").ap()
    poolw_d = nc.dram_tensor("pool_w", [NL, 4, 64, 64], F32, kind="ExternalInput").ap()
    ident_d = nc.dram_tensor("ident", [128, 128], F32, kind="ExternalInput").ap()
    invc_d = nc.dram_tensor("invc", [128, 32], F32, kind="ExternalInput").ap()
    y_d = nc.dram_tensor("y", [TMAIN, D], F32, kind="ExternalOutput").ap()

    trk = Trk()
    NTMP = 10
    from contextlib import ExitStack
    with ExitStack() as es:
        def sb(name, shape, dt):
            return es.enter_context(nc.sbuf_tensor(name, shape, dt))
        x32 = sb("x32", [128, NCH, TTOT], F32)
        win = sb("win", [128, NCH, 3072], BF16)
        wout = sb("wout", [128, NCH, D], BF16)
        dA = sb("dA", [128, 2 * KA, 128], BF16)
        dB = sb("dB", [128, 6, 128], BF16)
        PW = sb("PW", [128, 2, 128], BF16)
        WT = sb("WT", [128, 4, 128], BF16)
        Bavg = sb("Bavg", [128, 128], BF16)
        ones1k = sb("ones1k", [128, 128], BF16)
        ident = sb("ident_sb", [128, 128], F32)
        prmT = sb("prmT", [128, NL, 128], F32)
        dvb = sb("dvb_sb", [128, 3 * 256], F32)
        sgub = sb("sgub_sb", [128, 2, 128], F32)
        invc = sb("invc_sb", [128, 32], F32)
        epsc = sb("epsc", [128, 1], F32)
        xbfs = [sb("xbf%d" % i, [128, NCH, TW], BF16) for i in range(NXBF)]
        mixT = [sb("mixT%d" % i, [128, NCH, TW], BF16) for i in range(2)]
        abf0 = sb("abf0", [128, 2, 30 + TW], BF16)
        chb0 = sb("chb0", [128, 2, 2 + TW], BF16)
        hcb0 = sb("hcb0", [128, 2, 16 + TW], F32)
        pooled = sb("pooled", [128, 2, TW], BF16)
        st6 = sb("st6", [128, 2, 8], F32)
        mv = sb("mv", [128, 4, 2], F32)
        fx = sb("fx", [128, 16], F32)
        cst = sb("cst", [128, 4], F32)
        hb = sb("hb", [128, NL, 2], F32)
        tmps = [sb("tmp%d" % i, [128, 2, TW], F32) for i in range(NTMP)]
        banks = [es.enter_context(nc.psum_tensor("bank%d" % i, [128, 512], F32)) for i in range(8)]
        sems = {e: es.enter_context(nc.semaphore("sem_" + e)) for e in Trk.ENGS}
        dsems = [es.enter_context(nc.semaphore("dsem%d" % i)) for i in range(72)]

        tpool = Pool_(list(range(NTMP)))
        bpool = Pool_(list(range(8)))

        def TK(i):
            return ("tmp", i)

        def BK(i):
            return ("bank", i)

        trk.op("sp", lambda e: e.dma_start(out=ident[:], in_=ident_d[:, :]), writes=["ident"], dma="c0")
        trk.op("sp", lambda e: e.dma_start(out=invc[:], in_=invc_d[:, :]), writes=["invc"], dma="c1")
        trk.op("pool", lambda e: e.memset(Bavg[:], 0.0), writes=["Bavg"])
        trk.op("pool", lambda e: e.memset(Bavg[0:64, 0:64], 1.0 / 64), writes=["Bavg"])
        trk.op("pool", lambda e: e.memset(Bavg[64:128, 64:128], 1.0 / 64), writes=["Bavg"])
        trk.op("pool", lambda e: e.memset(ones1k[:], 1.0 / 1024), writes=["ones1k"])
        trk.op("pool", lambda e: e.memset(epsc[:], EPS), writes=["epsc"])
        trk.op("pool", lambda e: e.memset(cst[:], -0.5), writes=["cst"])
        for l in range(NL):
            t = tpool.get()
            trk.op("sp", lambda e, l=l, t=t: e.dma_start(out=tmps[t][:, 0, 0:128], in_=prm_d[l]),
                   writes=[TK(t)], dma="prm%d" % l)
            b = bpool.get()
            trk.op("pe", lambda e, t=t, b=b: e.transpose(banks[b][:, 0:128], tmps[t][:, 0, 0:128], ident[:]),
                   reads=[TK(t), "ident"], writes=[BK(b)])
            trk.op("dve", lambda e, l=l, b=b: e.tensor_copy(prmT[:, l, :], banks[b][:, 0:128]),
                   reads=[BK(b)], writes=[("prmT", l)])
            tpool.put(t)
            bpool.put(b)
            trk.op("dve", lambda e, l=l: e.tensor_scalar(hb[:, l, :], prmT[:, l, R_BIN + 2 * S_AGLU:R_BIN + 2 * S_AGLU + 2],
                                                        0.5, None, ALU.mult),
                   reads=[("prmT", l)], writes=[("hb", l)])

        def col(l, r):
            return prmT[:, l, r:r + 1]

        nblk = TTOT // 128
        stg = [mixT[0], xbfs[1], mixT[1], xbfs[2]]
        stgk = [("mixT", 0), ("xbf", 1), ("mixT", 1), ("xbf", 2)]
        STG_SEQ = [0, 1, 2, 3, 0, 1, 2, 3] + [2, 3] * 5
        stgv = [m_[:].rearrange("p a b -> p (a b)").bitcast(F32) for m_ in stg]
        for bI in range(nblk):
            s = STG_SEQ[bI]
            trk.op("sp", lambda e, bI=bI, s=s: e.dma_start(
                out=stgv[s], in_=x_d[bI * 128:(bI + 1) * 128, :]),
                writes=[stgk[s]], dma="xs%d" % s)
            for half in range(2):
                b = bpool.get()
                for q in range(4):
                    m = half * 4 + q
                    trk.op("pe", lambda e, s=s, m=m, q=q, b=b: e.transpose(
                        banks[b][:, q * 128:(q + 1) * 128],
                        stgv[s][:, m * 128:(m + 1) * 128], ident[:]),
                        reads=[stgk[s], "ident"], writes=[BK(b)])
                eng = "act" if half == 0 else "dve"
                if eng == "act":
                    trk.op("act", lambda e, b=b, half=half, bI=bI: e.activation(
                        out=x32[:, half * 4:(half + 1) * 4, bI * 128:(bI + 1) * 128],
                        in_=banks[b][:].rearrange("p (q k) -> p q k", k=128), func=AF.Identity),
                        reads=[BK(b)], writes=[("x", m, bI) for m in range(half * 4, half * 4 + 4)])
                else:
                    trk.op("dve", lambda e, b=b, half=half, bI=bI: e.tensor_copy(
                        x32[:, half * 4:(half + 1) * 4, bI * 128:(bI + 1) * 128],
                        banks[b][:].rearrange("p (q k) -> p q k", k=128)),
                        reads=[BK(b)], writes=[("x", m, bI) for m in range(half * 4, half * 4 + 4)])
                bpool.put(b)

        win_src = [win_d[l].rearrange("(kc p) e -> p kc e", p=128) for l in range(NL)]
        wout_src = [wout_d[l].rearrange("(kc p) e -> p kc e", p=128) for l in range(NL)]

        def load_win_slice(l, s):
            trk.op("pool", lambda e: e.dma_start(out=win[:, :, s * 256:(s + 1) * 256],
                                                 in_=win_src[l][:, :, s * 256:(s + 1) * 256]),
                   writes=[("win", s)], dma="win%d" % s, lat=9.0)

        def load_wout(l):
            for kc in range(NCH):
                trk.op("pool", lambda e, kc=kc: e.dma_start(out=wout[:, kc, :], in_=wout_src[l][:, kc, :]),
                       writes=[("wout", kc)], dma="wout%d" % kc)

        def load_layer_consts(l):
            for k in range(KA):
                for c in range(2):
                    i = 2 * k + c
                    if i % 2 == 0:
                        trk.op("act", lambda e, i=i: e.activation(out=dA[:, i, :], in_=ident[:], func=AF.Identity,
                                                                 scale=col(l, R_CAW + i)),
                               reads=["ident", ("prmT", l)], writes=[("dA", i)])
                    else:
                        trk.op("dve", lambda e, i=i: e.tensor_scalar(dA[:, i, :], ident[:], col(l, R_CAW + i), None,
                                                                    ALU.mult),
                               reads=["ident", ("prmT", l)], writes=[("dA", i)])
            for i in range(6):
                trk.op("dve", lambda e, i=i: e.tensor_scalar(dB[:, i, :], ident[:], col(l, R_CBW + i), None, ALU.mult),
                       reads=["ident", ("prmT", l)], writes=[("dB", i)])
            trk.op("pool", lambda e: e.memset(PW[:], 0.0), writes=["PW"])
            for g in range(4):
                c, h = g // 2, g % 2
                trk.op("pool", lambda e, g=g, c=c, h=h: e.dma_start(
                    out=PW[h * 64:(h + 1) * 64, c, h * 64:(h + 1) * 64], in_=poolw_d[l, g]),
                    writes=["PW"], dma="pw")
            for h in range(4):
                trk.op("pool", lambda e, h=h: e.dma_start(out=WT[:, h, :], in_=sguw_d[l, h]),
                       writes=["WT"], dma="wt")
            trk.op("pool", lambda e: e.memset(WT[64:128, :, 0:64], 0.0), writes=["WT"])
            trk.op("sp", lambda e: e.dma_start(out=dvb[:], in_=dvb_d[l:l + 1, :].partition_broadcast(128)),
                   writes=["dvb"], dma="dvb")
            for h in range(4):
                c, hh = h // 2, h % 2
                trk.op("sp", lambda e, h=h, c=c, hh=hh: e.dma_start(
                    out=sgub[hh * 64:(hh + 1) * 64, c, :], in_=sgub_d[l, h:h + 1, :].partition_broadcast(64)),
                    writes=["sgub"], dma="sgub")

        def bias(l, s, c):
            return col(l, R_BIN + 2 * s + c)

        def inproj(l, s, w, b, xi):
            for c in range(2):
                for kc in range(NCH):
                    trk.op("pe", lambda e, c=c, kc=kc: e.matmul(
                        banks[b][:, c * 256:c * 256 + w],
                        win[:, kc, s * 256 + c * 128:s * 256 + (c + 1) * 128],
                        xbfs[xi][:, kc, 0:w], start=(kc == 0), stop=(kc == NCH - 1)),
                        reads=[("win", s), ("xbf", xi)], writes=[BK(b)])

        def bv(b, w):
            return banks[b][:].rearrange("p (c t) -> p c t", c=2)[:, :, 0:w]

        def tv(t, w):
            return tmps[t][:, :, 0:w]

        def flat(t):
            return tmps[t][:].rearrange("p a b -> p (a b)")

        def bfv(t):
            return flat(t).bitcast(BF16)

        class GS:
            pass

        def gen_tile(st, l, t0, w, par, kind, first_in_layer, last_layer, uid):
            nb = w // 128
            blks = [t0 // 128 + j for j in range(nb)]
            xkeys = [("x", m, bI) for m in range(NCH) for bI in blks]
            A = abf0; C_ = chb0; H = hcb0
            xi = uid % NXBF
            KAH, KAC, KCH, KCC, KHH, KHC = "abfH", "abfC", "chbH", "chbC", "hcbH", "hcbC"
            mx = mixT[par]
            mk = ("mixT", par)
            xk_ = ("xbf", uid % NXBF)
            entering_main = (t0 + w == HALO)

            def act(out, in_, func, bias_=None, scale=None, rd=(), wr=()):
                kw = {}
                if bias_ is not None:
                    kw["bias"] = bias_
                if scale is not None:
                    kw["scale"] = scale
                cst_ = c_act(fsz(out))
                tbl_ = "sqrt" if func == AF.Sqrt else ("silu" if func in (AF.Silu, AF.Tanh) else None)
                trk.op("act", lambda e: e.activation(out=out, in_=in_, func=func, **kw), reads=list(rd), writes=list(wr), cost=cst_, tbl=tbl_)

            def tt(eng, out, a, b_, op, rd, wr):
                n_ = fsz(out)
                cst_ = c_dve(n_) if eng == "dve" else (0.17 * n_ + 0.3 if op == ALU.pow else c_pool(n_))
                trk.op(eng, lambda e: e.tensor_tensor(out, a, b_, op), reads=list(rd), writes=list(wr), cost=cst_)

            def stt(eng, out, a, sc, b_, op0, op1, rd, wr):
                n_ = fsz(out)
                cst_ = c_dve(n_) if eng == "dve" else c_pool(n_)
                trk.op(eng, lambda e: e.scalar_tensor_tensor(out, a, sc, b_, op0, op1), reads=list(rd), writes=list(wr), cost=cst_)

            P_ = ("prmT", l)
            if first_in_layer:
                trk.op("pool", lambda e: e.memset(A[:, :, 0:30], 0.0), writes=[KAH])
                trk.op("pool", lambda e: e.memset(C_[:, :, 0:2], 0.0), writes=[KCH])
                trk.op("pool", lambda e: e.memset(H[:, :, 0:16], 0.0), writes=[KHH])
            if CAST_POOL:
                trk.op("pool", lambda e: e.tensor_copy(xbfs[xi][:, :, 0:w], x32[:, :, t0:t0 + w]),
                       reads=xkeys, writes=[xk_], cost=0.0038 * 8 * w)
            else:
                act(xbfs[xi][:, :, 0:w], x32[:, :, t0:t0 + w], AF.Identity, rd=xkeys, wr=[xk_])
            yield
            b_glu = bpool.get(); inproj(l, S_AGLU, w, b_glu, xi)
            b_val = bpool.get(); inproj(l, S_AVAL, w, b_val, xi)
            t_sig = tpool.get()
            for c in range(2):
                act(tmps[t_sig][:, c, 0:w], banks[b_glu][:, c * 256:c * 256 + w], AF.Tanh, bias_=hb[:, l, c:c + 1], scale=0.5,
                    rd=[BK(b_glu), ("hb", l)], wr=[TK(t_sig)])
            bpool.put(b_glu)
            trk.op("dve", lambda e: e.tensor_scalar(tv(t_sig, w), tv(t_sig, w), 0.5, 0.5, ALU.mult, ALU.add),
                   reads=[TK(t_sig)], writes=[TK(t_sig)])
            for c in range(2):
                stt("dve", A[:, c, 30:30 + w], banks[b_val][:, c * 256:c * 256 + w], bias(l, S_AVAL, c), tmps[t_sig][:, c, 0:w],
                    ALU.add, ALU.mult, [BK(b_val), TK(t_sig), P_], [KAC])
            bpool.put(b_val); tpool.put(t_sig)
            yield
            b_bc = bpool.get(); inproj(l, S_BC, w, b_bc, xi)
            b_bh = bpool.get(); inproj(l, S_BH, w, b_bh, xi)
            t_bc = tpool.get()
            for c in range(2):
                act(tmps[t_bc][:, c, 0:w], banks[b_bc][:, c * 256:c * 256 + w], AF.Identity, bias_=bias(l, S_BC, c), scale=1.0,
                    rd=[BK(b_bc), P_], wr=[TK(t_bc)])
            bpool.put(b_bc)
            for c in range(2):
                stt("dve", C_[:, c, 2:2 + w], banks[b_bh][:, c * 256:c * 256 + w], bias(l, S_BH, c), tmps[t_bc][:, c, 0:w],
                    ALU.add, ALU.mult, [BK(b_bh), TK(t_bc), P_], [KCC])
            bpool.put(b_bh); tpool.put(t_bc)
            yield
            b_ch = bpool.get(); inproj(l, S_CH, w, b_ch, xi)
            if kind == "full":
                b_az = bpool.get(); inproj(l, S_AZ, w, b_az, xi)
            for c in range(2):
                act(H[:, c, 16:16 + w], banks[b_ch][:, c * 256:c * 256 + w], AF.Identity, bias_=bias(l, S_CH, c), scale=1.0,
                    rd=[BK(b_ch), P_], wr=[KHC])
            bpool.put(b_ch)

            def hand(dst, src, key_d, key_s):
                if entering_main:
                    act(dst, src, AF.Identity, scale=col(l, R_FLAG), rd=[key_s, P_], wr=[key_d])
                else:
                    trk.op("pool", lambda e: e.tensor_copy(dst, src), reads=[key_s], writes=[key_d])
            if kind == "hist":
                hand(A[:, :, 0:30], A[:, :, w:w + 30], KAH, KAC)
                hand(C_[:, :, 0:2], C_[:, :, w:w + 2], KCH, KCC)
                hand(H[:, :, 0:16], H[:, :, w:w + 16], KHH, KHC)
            if kind == "hist":
                st.win_done = True
                st.consts_done = True
                st.wout_done = True
                return
            t_sza = tpool.get()
            for c in range(2):
                act(tmps[t_sza][:, c, 0:w], banks[b_az][:, c * 256:c * 256 + w], AF.Silu, bias_=bias(l, S_AZ, c), scale=1.0,
                    rd=[BK(b_az), P_], wr=[TK(t_sza)])
            bpool.put(b_az)
            yield
            b_cv = bpool.get()
            for c in range(2):
                for k in range(KA):
                    trk.op("pe", lambda e, c=c, k=k: e.matmul(
                        banks[b_cv][:, c * 256:c * 256 + w], dA[:, 2 * k + c, :], A[:, c, k:k + w],
                        start=(k == 0), stop=(k == KA - 1)),
                        reads=[("dA", 2 * k + c), KAH, KAC], writes=[BK(b_cv)])
            hand(A[:, :, 0:30], A[:, :, w:w + 30], KAH, KAC)
            b_bb = bpool.get(); inproj(l, S_BB, w, b_bb, xi)
            t_c = tpool.get(); t_bq = tpool.get()
            trk.merge(("cb", uid), [TK(t_bq)])
            trk.merge(("csq", uid), [TK(t_bq)])
            cbv = bfv(t_bq)

            def cb_ap(c):
                return cbv[:, c * 256:c * 256 + w]

            def csq_ap(c):
                return cbv[:, 512 + c * 256:512 + c * 256 + w]
            for c in range(2):
                act(tmps[t_c][:, c, 0:w], banks[b_cv][:, c * 256:c * 256 + w], AF.Identity, bias_=col(l, R_CAB + c), scale=1.0,
                    rd=[BK(b_cv), P_], wr=[TK(t_c)])
                act(csq_ap(c), banks[b_cv][:, c * 256:c * 256 + w], AF.Square, bias_=col(l, R_CAB + c), scale=1.0,
                    rd=[BK(b_cv), P_], wr=[("csq", uid)])
                act(cb_ap(c), banks[b_cv][:, c * 256:c * 256 + w], AF.Identity, bias_=col(l, R_CAB + c), scale=1.0,
                    rd=[BK(b_cv), P_], wr=[("cb", uid)])
            bpool.put(b_cv)
            yield
            b_bz = bpool.get(); inproj(l, S_BZ, w, b_bz, xi)
            b_cb = bpool.get()
            for c in range(2):
                for k in range(3):
                    trk.op("pe", lambda e, c=c, k=k: e.matmul(
                        banks[b_cb][:, c * 256:c * 256 + w], dB[:, 2 * k + c, :], C_[:, c, k:k + w],
                        start=(k == 0), stop=(k == 2)),
                        reads=[("dB", 2 * k + c), KCH, KCC], writes=[BK(b_cb)])
            hand(C_[:, :, 0:2], C_[:, :, w:w + 2], KCH, KCC)
            t_szb = tpool.get()
            for c in range(2):
                act(tmps[t_szb][:, c, 0:w], banks[b_bz][:, c * 256:c * 256 + w], AF.Silu, bias_=bias(l, S_BZ, c), scale=1.0,
                    rd=[BK(b_bz), P_], wr=[TK(t_szb)])
            bpool.put(b_bz)
            for c in range(2):
                stt("dve", tmps[t_szb][:, c, 0:w], banks[b_bb][:, c * 256:c * 256 + w], bias(l, S_BB, c), tmps[t_szb][:, c, 0:w],
                    ALU.add, ALU.mult, [BK(b_bb), TK(t_szb), P_], [TK(t_szb)])
            bpool.put(b_bb)
            tt("dve", mx[:, 2:4, 0:w], tv(t_szb, w), bv(b_cb, w), ALU.mult, [TK(t_szb), BK(b_cb)], [mk])
            bpool.put(b_cb); tpool.put(t_szb)
            hk = "hcbHC"
            W_ = 16 + w
            t_pa = tpool.get(); t_pb = tpool.get()
            pa = flat(t_pa); pb = flat(t_pb)
            ka, kb = TK(t_pa), TK(t_pb)
            for c in range(2):
                Hc = H[:, c, :]
                tt(POOL_ENG, pa[:, 1:W_], Hc[:, 1:W_], Hc[:, 0:W_ - 1], ALU.add, [KHH, KHC], [ka])
                tt(POOL_ENG, pb[:, 3:W_], pa[:, 3:W_], pa[:, 1:W_ - 2], ALU.add, [ka], [kb])
                if c == 1:
                    tt(POOL_ENG, pa[:, 7:W_], pb[:, 7:W_], pb[:, 3:W_ - 4], ALU.add, [kb], [ka])
                    tt(POOL_ENG, pb[:, 15:W_], pa[:, 15:W_], pa[:, 7:W_ - 8], ALU.add, [ka], [kb])
                pinv = col(l, R_PINV + c)
                stt("dve", pooled[0:64, c, 0:w], pa[0:64, 16:W_], pinv[0:64], H[0:64, c, 16:W_], ALU.mult, ALU.subtract,
                    [ka, KHH, KHC, P_], ["pooled"])
                stt("dve", pooled[64:128, c, 0:w], pb[64:128, 16:W_], pinv[64:128], H[64:128, c, 16:W_], ALU.mult, ALU.subtract,
                    [kb, KHH, KHC, P_], ["pooled"])
                if t0 == HALO:
                    for (src, sk, p0) in ((pa, ka, 0), (pb, kb, 64)):
                        tt("dve", fx[p0:p0 + 64, 0:16], src[p0:p0 + 64, 16:32], invc[p0:p0 + 64, c * 16:(c + 1) * 16],
                           ALU.mult, [sk, "invc"], ["fx"])
                        tt("dve", pooled[p0:p0 + 64, c, 0:16], fx[p0:p0 + 64, 0:16], H[p0:p0 + 64, c, 16:32],
                           ALU.subtract, ["fx", KHH, KHC], ["pooled"])
            tpool.put(t_pa); tpool.put(t_pb)
            hand(H[:, :, 0:16], H[:, :, w:w + 16], KHH, KHC)
            yield
            b_cz = bpool.get(); inproj(l, S_CZ, w, b_cz, xi)
            b_dv = bpool.get()
            for j in range(nb):
                for kc in range(NCH):
                    trk.op("pe", lambda e, j=j, kc=kc: e.matmul(
                        banks[b_dv][:, j * 256:(j + 1) * 256], xbfs[xi][:, kc, j * 128:(j + 1) * 128],
                        win[:, kc, S_DV * 256:(S_DV + 1) * 256], start=(kc == 0), stop=(kc == NCH - 1)),
                        reads=[("win", S_DV), xk_], writes=[BK(b_dv)])
            b_mean = bpool.get(); b_ex2 = bpool.get()
            for c in range(2):
                trk.op("pe", lambda e, c=c: e.matmul(banks[b_mean][:, c * 256:c * 256 + w], Bavg[:], cb_ap(c),
                                                     start=True, stop=True),
                       reads=["Bavg", ("cb", uid)], writes=[BK(b_mean)])
            for c in range(2):
                trk.op("pe", lambda e, c=c: e.matmul(banks[b_ex2][:, c * 256:c * 256 + w], Bavg[:], csq_ap(c),
                                                     start=True, stop=True),
                       reads=["Bavg", ("csq", uid)], writes=[BK(b_ex2)])
            trk.merge(TK(t_bq), [("cb", uid), ("csq", uid)])
            tpool.put(t_bq)
            t_szc = tpool.get()
            for c in range(2):
                act(tmps[t_szc][:, c, 0:w], banks[b_cz][:, c * 256:c * 256 + w], AF.Silu, bias_=bias(l, S_CZ, c), scale=1.0,
                    rd=[BK(b_cz), P_], wr=[TK(t_szc)])
            bpool.put(b_cz)
            t_v = tpool.get(); t_vb = tpool.get()
            vbv = bfv(t_vb)
            db3 = dvb[:, 0:256].unsqueeze(1)
            if nb > 1:
                db3 = db3.to_broadcast([128, nb, 256])
            tt("dve", tmps[t_v][:, 0:nb, :], banks[b_dv][:].rearrange("p (j c) -> p j c", j=2)[:, 0:nb, :], db3, ALU.add,
               [BK(b_dv), "dvb"], [TK(t_v)])
            bpool.put(b_dv)
            for j in range(nb):
                trk.op("dve", lambda e, j=j: e.bn_stats(st6[:, j, 0:6], tmps[t_v][:, j, :]), reads=[TK(t_v)], writes=[("st6", j)])
                trk.op("dve", lambda e, j=j: e.bn_aggr(mv[:, 0:2, j], st6[:, j, 0:6]), reads=[("st6", j)], writes=["mv01"])
            trk.op("dve", lambda e: e.tensor_scalar(mv[:, 2, 0:nb], mv[:, 1, 0:nb], EPS, None, ALU.add),
                   reads=["mv01"], writes=["mv2"])
            tt("pool", mv[:, 2, 0:nb], mv[:, 2, 0:nb], cst[:, 2:2 + nb], ALU.pow, ["mv2", "cst"], ["mv2"])
            for j in range(nb):
                stt("dve", tmps[t_v][:, j, :], tmps[t_v][:, j, :], mv[:, 0, j:j + 1], dvb[:, 256:512], ALU.subtract, ALU.mult,
                    [TK(t_v), "mv01", "dvb"], [TK(t_v)])
            for j in range(nb):
                stt("dve", vbv[:, j * 256:(j + 1) * 256], tmps[t_v][:, j, :], mv[:, 2, j:j + 1], dvb[:, 512:768], ALU.mult, ALU.add,
                    [TK(t_v), "mv2", "dvb"], [TK(t_vb)])
            tpool.put(t_v)
            t_ra = tpool.get()
            act(tv(t_ra, w), bv(b_mean, w), AF.Square, rd=[BK(b_mean)], wr=[TK(t_ra)])
            stt("dve", tv(t_ra, w), bv(b_ex2, w), EPS, tv(t_ra, w), ALU.add, ALU.subtract, [BK(b_ex2), TK(t_ra)], [TK(t_ra)])
            bpool.put(b_ex2)
            act(tv(t_ra, w), tv(t_ra, w), AF.Sqrt, rd=[TK(t_ra)], wr=[TK(t_ra)])
            trk.op("dve", lambda e: e.reciprocal(tv(t_ra, w), tv(t_ra, w)), reads=[TK(t_ra)], writes=[TK(t_ra)], cost=0.0066 * 2 * w + 0.1)
            tt("dve", tv(t_c, w), tv(t_c, w), bv(b_mean, w), ALU.subtract, [TK(t_c), BK(b_mean)], [TK(t_c)])
            bpool.put(b_mean)
            yield
            b_du = bpool.get(); inproj(l, S_DU, w, b_du, xi)
            b_dz = bpool.get(); inproj(l, S_DZ, w, b_dz, xi)
            st.win_done = True
            t_du = tpool.get(); t_szd = tpool.get()
            for c in range(2):
                act(tmps[t_du][:, c, 0:w], banks[b_du][:, c * 256:c * 256 + w], AF.Identity, bias_=bias(l, S_DU, c), scale=1.0,
                    rd=[BK(b_du), P_], wr=[TK(t_du)])
                act(tmps[t_szd][:, c, 0:w], banks[b_dz][:, c * 256:c * 256 + w], AF.Silu, bias_=bias(l, S_DZ, c), scale=1.0,
                    rd=[BK(b_dz), P_], wr=[TK(t_szd)])
            bpool.put(b_du); bpool.put(b_dz)
            tt("pool", tv(t_c, w), tv(t_c, w), tv(t_ra, w), ALU.mult, [TK(t_c), TK(t_ra)], [TK(t_c)])
            tpool.put(t_ra)
            for c in range(2):
                act(tmps[t_c][:, c, 0:w], tmps[t_c][:, c, 0:w], AF.Silu, bias_=col(l, R_NAB + c), scale=col(l, R_NAG + c),
                    rd=[TK(t_c), P_], wr=[TK(t_c)])
            tt(YA_ENG, mx[:, 0:2, 0:w], tv(t_c, w), tv(t_sza, w), ALU.mult, [TK(t_c), TK(t_sza)], [mk])
            tpool.put(t_c); tpool.put(t_sza)
            tt("pool", tv(t_du, w), tv(t_du, w), tv(t_szd, w), ALU.mult, [TK(t_du), TK(t_szd)], [TK(t_du)])
            tpool.put(t_szd)
            yield
            b_pw = bpool.get()
            for c in range(2):
                trk.op("pe", lambda e, c=c: e.matmul(banks[b_pw][:, c * 256:c * 256 + w], PW[:, c, :], pooled[:, c, 0:w],
                                                     start=True, stop=True),
                       reads=["PW", "pooled"], writes=[BK(b_pw)])
            for c in range(2):
                stt("dve", mx[:, 4 + c, 0:w], banks[b_pw][:, c * 256:c * 256 + w], col(l, R_PSC + c), tmps[t_szc][:, c, 0:w],
                    ALU.mult, ALU.mult, [BK(b_pw), TK(t_szc), P_], [mk])
            bpool.put(b_pw); tpool.put(t_szc)
            yield
            b_sp = bpool.get()
            for j in range(nb):
                for h in range(4):
                    c, hh = h // 2, h % 2
                    trk.op("pe", lambda e, j=j, h=h, c=c, hh=hh: e.matmul(
                        banks[b_sp][hh * 64:(hh + 1) * 64, c * 256 + j * 128:c * 256 + (j + 1) * 128],
                        vbv[:, j * 256 + h * 64:j * 256 + (h + 1) * 64], WT[:, h, :], start=True, stop=True),
                        reads=[TK(t_vb), "WT"], writes=[BK(b_sp)])
            st.consts_done = True
            tpool.put(t_vb)
            t_sp = tpool.get()
            sg4 = sgub[:].unsqueeze(2)
            if nb > 1:
                sg4 = sg4.to_broadcast([128, 2, nb, 128])
            tt("dve", tv(t_sp, w).rearrange("p c (j i) -> p c j i", i=128),
               bv(b_sp, w).rearrange("p c (j i) -> p c j i", i=128), sg4, ALU.add,
               [BK(b_sp), "sgub"], [TK(t_sp)])
            bpool.put(b_sp)
            tt(YD_ENG, mx[:, 6:8, 0:w], tv(t_sp, w), tv(t_du, w), ALU.mult, [TK(t_sp), TK(t_du)], [mk])
            tpool.put(t_sp); tpool.put(t_du)
            yield
            yield
            assert wout_loaded[0] == l, "w_out for layer %d not loaded yet" % l
            b_mu = bpool.get()
            t_zb = tpool.get(); t_zq = tpool.get()
            zbv = bfv(t_zb); zqv = bfv(t_zq)
            for i_ in range(4):
                trk.merge(("zb", uid, i_), [TK(t_zb)])
                trk.merge(("zq", uid, i_), [TK(t_zq)])

            def zb_ap(m):
                return zbv[:, (m % 4) * 256:(m % 4) * 256 + w]

            def zq_ap(m):
                return zqv[:, (m % 4) * 256:(m % 4) * 256 + w]

            def stats(m):
                trk.op("pe", lambda e: e.matmul(banks[b_mu][:, 0:w], ones1k[:], zb_ap(m), start=(m == 0), stop=(m == NCH - 1),
                                                skip_group_check=True),
                       reads=["ones1k", ("zb", uid, m % 4)], writes=[BK(b_mu)])
                trk.op("pe", lambda e: e.matmul(banks[b_mu][:, 256:256 + w], ones1k[:], zq_ap(m), start=False, stop=(m == NCH - 1),
                                                skip_group_check=True),
                       reads=["ones1k", ("zq", uid, m % 4)], writes=[BK(b_mu)])
            for mp in range(4):
                b_o = bpool.get()
                for mm in range(2):
                    m = 2 * mp + mm
                    for kc in range(NCH):
                        trk.op("pe", lambda e, m=m, mm=mm, kc=kc, b_o=b_o: e.matmul(
                            banks[b_o][:, mm * 256:mm * 256 + w], wout[:, kc, m * 128:(m + 1) * 128], mx[:, kc, 0:w],
                            start=(kc == 0), stop=(kc == NCH - 1)),
                            reads=[("wout", kc), mk], writes=[BK(b_o)])
                if mp == 3:
                    st.wout_done = True
                if mp >= 2:
                    for m_ in (2 * mp - 4, 2 * mp - 3):
                        stats(m_)
                t_z = tpool.get()
                for mm in range(2):
                    m = 2 * mp + mm
                    act(tmps[t_z][:, mm, 0:w], banks[b_o][:, mm * 256:mm * 256 + w], AF.Identity, bias_=col(l, R_BOUT + m), scale=1.0,
                        rd=[BK(b_o), P_], wr=[TK(t_z)])
                bpool.put(b_o)
                xk2 = [("x", m_, bI) for m_ in (2 * mp, 2 * mp + 1) for bI in blks]
                xs2 = x32[:, 2 * mp:2 * mp + 2, t0:t0 + w]
                s0 = (2 * mp) % 4
                zq2 = zqv[:, s0 * 256:(s0 + 2) * 256].rearrange("p (a b) -> p a b", a=2)[:, :, 0:w]
                zb2 = zbv[:, s0 * 256:(s0 + 2) * 256].rearrange("p (a b) -> p a b", a=2)[:, :, 0:w]
                stt("dve", xs2, xs2, ALPHA, tmps[t_z][:, 0:2, 0:w], ALU.mult, ALU.add, xk2 + [TK(t_z)], xk2)
                tt("dve", zq2, xs2, xs2, ALU.mult, xk2, [("zq", uid, s0), ("zq", uid, s0 + 1)])
                if ZB_ACT:
                    act(zb2, xs2, AF.Identity, rd=xk2, wr=[("zb", uid, s0), ("zb", uid, s0 + 1)])
                else:
                    trk.op("dve", lambda e, zb2=zb2, xs2=xs2: e.tensor_copy(zb2, xs2),
                           reads=xk2, writes=[("zb", uid, s0), ("zb", uid, s0 + 1)], cost=0.35)
                tpool.put(t_z)
                yield
            stats(4); stats(5)
            yield
            stats(6); stats(7)
            trk.merge(TK(t_zb), [("zb", uid, i) for i in range(4)])
            trk.merge(TK(t_zq), [("zq", uid, i) for i in range(4)])
            tpool.put(t_zb); tpool.put(t_zq)
            t_r = tpool.get()
            R = tmps[t_r]
            act(R[:, 0, 0:w], banks[b_mu][:, 0:w], AF.Square, rd=[BK(b_mu)], wr=[TK(t_r)])
            stt("dve", R[:, 0, 0:w], banks[b_mu][:, 256:256 + w], EPS, R[:, 0, 0:w], ALU.add, ALU.subtract, [BK(b_mu), TK(t_r)], [TK(t_r)])
            act(R[:, 0, 0:w], R[:, 0, 0:w], AF.Sqrt, rd=[TK(t_r)], wr=[TK(t_r)])
            trk.op("dve", lambda e: e.reciprocal(R[:, 0, 0:w], R[:, 0, 0:w]), reads=[TK(t_r)], writes=[TK(t_r)], cost=0.0066 * w + 0.1)
            stt("dve", R[:, 1, 0:w], banks[b_mu][:, 0:w], -1.0, R[:, 0, 0:w], ALU.mult, ALU.mult, [BK(b_mu), TK(t_r)], [TK(t_r)])
            bpool.put(b_mu)
            yield
            for (eng, m0) in ((("dve", 0), (APPLY_HI_ENG, 4))):
                xk4 = [("x", m_, bI) for m_ in range(m0, m0 + 4) for bI in blks]
                xs4 = x32[:, m0:m0 + 4, t0:t0 + w]
                tt(eng, xs4, xs4, R[:, 0:1, 0:w].to_broadcast([128, 4, w]), ALU.mult, xk4 + [TK(t_r)], xk4)
                tt(eng, xs4, xs4, R[:, 1:2, 0:w].to_broadcast([128, 4, w]), ALU.add, xk4 + [TK(t_r)], xk4)
                for m in range(m0, m0 + 4):
                    xk = [("x", m, bI) for bI in blks]
                    act(x32[:, m, t0:t0 + w], x32[:, m, t0:t0 + w], AF.Identity, bias_=col(l, R_LNB + m), scale=col(l, R_LNG + m),
                        rd=xk + [P_], wr=xk)
                yield
            tpool.put(t_r)
            if last_layer:
                for j in range(nb):
                    bI = blks[j]
                    touts = [tpool.get(), tpool.get()]
                    for half in range(2):
                        b = bpool.get()
                        for q in range(4):
                            m = half * 4 + q
                            trk.op("pe", lambda e, m=m, q=q, b=b, bI=bI: e.transpose(
                                banks[b][:, q * 128:(q + 1) * 128], x32[:, m, bI * 128:(bI + 1) * 128], ident[:]),
                                reads=[("x", m, bI), "ident"], writes=[BK(b)])
                        tki = touts[half]
                        dst = flat(tki)
                        if half == 0:
                            act(dst, banks[b][:], AF.Identity, rd=[BK(b)], wr=[TK(tki)])
                        else:
                            trk.op("dve", lambda e, b=b, dst=dst: e.tensor_copy(dst, banks[b][:]),
                                   reads=[BK(b)], writes=[TK(tki)])
                        bpool.put(b)
                        r0 = bI * 128 - HALO
                        trk.op("sp", lambda e, dst=dst, r0=r0, half=half: e.dma_start(
                            out=y_d[r0:r0 + 128, half * 512:(half + 1) * 512], in_=dst),
                            reads=[TK(tki)], writes=[("y", bI, half)], dma="y%d_%d" % (bI, half))
                    tpool.put(touts[0]); tpool.put(touts[1])
                    yield

        starts_all = [("full", 0, None), ("hist", 0, 128), ("full", 128, None), ("hist", 128, 256)]
        sched = starts_all[4 - NL:]
        tiles = []
        for l in range(NL):
            kind, s0, full_from = sched[l]
            if kind == "hist":
                tiles.append((l, s0, 128, "hist"))
                pos = full_from
            else:
                pos = s0
            while pos < TTOT:
                w = 128 if pos % 256 == 128 else 256
                tiles.append((l, pos, w, "full"))
                pos += w
        if n_tiles_dbg is not None:
            tiles = tiles[:n_tiles_dbg]

        WIN_ORDER = (S_AGLU, S_AVAL, S_BC, S_BH, S_CH, S_AZ, S_BB, S_BZ, S_CZ, S_DV, S_DU, S_DZ)
        wout_loaded = [0]
        for s in WIN_ORDER:
            load_win_slice(0, s)
        load_layer_consts(0)
        load_wout(0)

        active = []
        LAG = LAG_

        def step(gs):
            try:
                next(gs.gen)
                gs.stage += 1
                return True
            except StopIteration:
                return False

        def step_all():
            for gs in list(active):
                if not step(gs):
                    active.remove(gs)
            maybe_wout()

        pend_wout = [None]

        def maybe_wout():
            l = pend_wout[0]
            if l is None:
                return
            if all(g.wout_done for g in active if g.l < l):
                load_wout(l)
                wout_loaded[0] = l
                pend_wout[0] = None

        par = 0
        for ti, (l, t0, w, kind) in enumerate(tiles):
            first = (ti == 0) or (tiles[ti - 1][0] != l)
            if first and l > 0:
                while any((not g.consts_done) for g in active if g.l < l):
                    step_all()
                for s in WIN_ORDER:
                    load_win_slice(l, s)
                load_layer_consts(l)
                pend_wout[0] = l
                maybe_wout()
            while len(active) >= NACTIVE_ or (active and active[-1].stage < LAG):
                step_all()
            gs = GS()
            gs.l = l
            gs.stage = 0
            gs.win_done = False
            gs.consts_done = False
            gs.wout_done = False
            gs.gen = gen_tile(gs, l, t0, w, par, kind, first, l == NL - 1, ti)
            active.append(gs)
            par = 1 - par
        while active:
            step_all()
        ykeys = [k for k in trk.last_w if isinstance(k, tuple) and k[0] == "y"]
        trk.op("sp", lambda e: e.nop(), reads=ykeys)

        trk.finalize(reorder=REORDER, window=WINDOW_)
        with nc.Block() as block:
            engs = {}

            def mk_runner(name):
                def run(e):
                    engs[name] = e
                return run
            order = {"pe": "tensor", "act": "scalar", "dve": "vector", "pool": "gpsimd", "sp": "sync"}
            def emit_engine(name):
                def f(e):
                    for o in trk.eng_ops[name]:
                        for d in o.waits:
                            if d.is_dma:
                                e.wait_ge(dsems[trk.dma_sems[d.dsem][0]], d.dval)
                            else:
                                e.wait_ge(sems[d.eng], d.val)
                        ins = o.fn(e)
                        if o.is_dma:
                            ins.then_inc(dsems[trk.dma_sems[o.dsem][0]], 16)
                        elif o.target:
                            ins.then_inc(sems[name], 1)
                return f
            block.tensor(emit_engine("pe"))
            block.scalar(emit_engine("act"))
            block.vector(emit_engine("dve"))
            block.gpsimd(emit_engine("pool"))
            block.sync(emit_engine("sp"))
    nc._trk_stats = {e: len(trk.eng_ops[e]) for e in Trk.ENGS}
    nc._trk_stats['makespan'] = trk.makespan
    nc._trk_stats['act_switches'] = trk.n_switch
    nc._trk = trk
    return nc


def host_prep(inputs, NL=DEPTH):
    g = {k: np.asarray(v, dtype=np.float32) for k, v in inputs.items()}
    x = g["x"]
    ident = np.eye(128, dtype=np.float32)
    wins = (2, 4, 8, 16)
    pinv = np.zeros((2, 128), np.float32)
    for c in range(2):
        for h in range(2):
            pinv[c, h * 64:(h + 1) * 64] = 1.0 / wins[2 * c + h]
    in_maps = []
    Ls = list(range(DEPTH))[:NL]
    w_in = np.ascontiguousarray(g["w_in"][Ls])
    w_out = np.ascontiguousarray(g["w_out"][Ls])
    dvb = np.stack([np.concatenate([g["b_in"][l][S_DV * 256:(S_DV + 1) * 256], g["sgu_ln_g"][l], g["sgu_ln_b"][l]])
                    for l in Ls]).astype(np.float32)
    sguwT = np.ascontiguousarray(np.transpose(g["sgu_w"][Ls], (0, 1, 3, 2)))
    sgub = np.ascontiguousarray(g["sgu_bias"][Ls])
    poolw = np.ascontiguousarray(g["pool_w"][Ls])
    for core in range(8):
        b, half = core // 2, core % 2
        xs = np.zeros((TTOT, D), np.float32)
        if half == 0:
            xs[HALO:] = x[b, 0:TMAIN]
            flag = 0.0
        else:
            xs[:] = x[b, TMAIN - HALO:SEQ]
            flag = 1.0
        prm = np.zeros((NL, 128, 128), np.float32)
        for i, l in enumerate(Ls):
            prm[i, R_BIN:R_BIN + 24] = g["b_in"][l].reshape(24, 128)
            prm[i, R_CAW:R_CAW + 62] = g["conv_a_w"][l].reshape(62, 128)
            prm[i, R_CAB:R_CAB + 2] = g["conv_a_b"][l].reshape(2, 128)
            prm[i, R_NAG:R_NAG + 2] = g["norm_a_g"][l].reshape(2, 128)
            prm[i, R_NAB:R_NAB + 2] = g["norm_a_b"][l].reshape(2, 128)
            prm[i, R_CBW:R_CBW + 6] = g["conv_b_w"][l].reshape(6, 128)
            prm[i, R_PSC:R_PSC + 2] = g["pool_scale"][l].reshape(2, 128)
            prm[i, R_BOUT:R_BOUT + 8] = g["b_out"][l].reshape(8, 128)
            prm[i, R_LNG:R_LNG + 8] = g["ln_g"][l].reshape(8, 128)
            prm[i, R_LNB:R_LNB + 8] = g["ln_b"][l].reshape(8, 128)
            prm[i, R_PINV:R_PINV + 2] = pinv
            prm[i, R_FLAG] = flag
            prm[i, R_NEG1] = -1.0
        invc = np.zeros((128, 32), np.float32)
        for c in range(2):
            for h in range(2):
                wv = wins[2 * c + h]
                for t in range(16):
                    cnt = min(t + 1, wv) if half == 0 else wv
                    invc[h * 64:(h + 1) * 64, c * 16 + t] = 1.0 / cnt
        in_maps.append({"x": xs, "w_in": w_in, "w_out": w_out, "prm": prm, "dvb": dvb, "sguwT": sguwT,
                        "sgub": sgub, "pool_w": poolw, "ident": ident, "invc": invc})
    return in_maps


_NC_CACHE = {}


def kernel(**inputs):
    if "nc" not in _NC_CACHE:
        _NC_CACHE["nc"] = build(DEPTH)
    nc = _NC_CACHE["nc"]
    in_maps = host_prep(inputs, DEPTH)
    res = run_bass_kernel_spmd(nc, in_maps, core_ids=list(range(8)))
    out = np.zeros((BATCH, SEQ, D), np.float32)
    for core in range(8):
        b, half = core // 2, core % 2
        out[b, half * TMAIN:(half + 1) * TMAIN] = res.results[core]["y"]
    return out
```

```python
import numpy as np
import concourse.bass as bass
import concourse.mybir as mybir
from concourse.bass_utils import run_bass_kernel_spmd

F32 = mybir.dt.float32
BF16 = mybir.dt.bfloat16
AF = mybir.ActivationFunctionType
ALU = mybir.AluOpType

D = 1024
NCH = 8
SEQ = 4096
BATCH = 4
DEPTH = 4
TMAIN = 2048
HALO = 256
TTOT = TMAIN + HALO
TW = 256
KA = 31
EPS = 1e-5
ALPHA = float((2.0 * DEPTH) ** 0.25)
REORDER = True
import os as _os8
APPLY_HI_ENG = _os8.environ.get("K_APPLY_HI", "dve")
import os as _os7
CAST_POOL = _os7.environ.get("K_CAST_POOL", "0") == "1"
import os as _os6
TBL_PEN = float(_os6.environ.get("K_TBL_PEN", "0.0"))
TBL_HOLD = float(_os6.environ.get("K_TBL_HOLD", "0.0"))
import os as _os4
OVH_DVE = float(_os4.environ.get("K_OVH_DVE", "0.1"))
OVH_ACT = float(_os4.environ.get("K_OVH_ACT", "0.08"))
import os as _os3
ZB_ACT = _os3.environ.get("K_ZB_ACT", "1") == "1"
XNEW_ACT = _os3.environ.get("K_XNEW_ACT", "1") == "1"
APPLY_DVE = tuple(int(c) for c in _os3.environ.get("K_APPLY_DVE", "0246"))
SYNC_ALL = True
NXBF = 3
TAGS = False
import os as _os2
PRIO_BL = _os2.environ.get("K_PRIO_BL", "0") == "1"
SLOT_ = float(_os2.environ.get("K_SLOT", "0.3"))
import os as _os
ZSQ_ACT = _os.environ.get("K_ZSQ_ACT", "0") == "1"
POOL_ENG = _os.environ.get("K_POOL_ENG", "pool")
YA_ENG = _os.environ.get("K_YA_ENG", "dve")
YD_ENG = _os.environ.get("K_YD_ENG", "dve")
LAG_ = int(_os.environ.get("K_LAG", "6"))
WINDOW_ = int(_os.environ.get("K_WINDOW", "96"))
NACTIVE_ = int(_os.environ.get("K_NACTIVE", "3"))

R_BIN = 0
R_CAW = 24
R_CAB = 86
R_NAG = 88
R_NAB = 90
R_CBW = 92
R_PSC = 98
R_BOUT = 100
R_LNG = 108
R_LNB = 116
R_PINV = 124
R_FLAG = 126
R_NEG1 = 127

(S_AVAL, S_AGLU, S_AZ, S_BB, S_BC, S_BH, S_BZ, S_CH, S_CZ, S_DU, S_DV, S_DZ) = range(12)


class Op:
    __slots__ = ("id", "eng", "fn", "waits", "is_dma", "dsem", "dval", "seq", "clock", "target", "val",
                 "preds", "cost", "lat", "fin", "start", "tag", "tbl")


OVH = {"dve": OVH_DVE, "act": OVH_ACT, "pool": 0.1}
DEF_COST = {"pe": 0.118, "act": 0.42, "dve": 0.3, "pool": 0.8, "sp": 0.1}


class Trk:
    ENGS = ("pe", "act", "dve", "pool", "sp")

    def __init__(self):
        self.ops = []
        self.eng_ops = {e: [] for e in self.ENGS}
        self.last_w = {}
        self.readers = {}
        self.dma_sems = {}
        self.n_dma_sem = 0

    def op(self, eng, fn, reads=(), writes=(), dma=None, cost=None, lat=0.0, tbl=None):
        o = Op()
        o.tbl = tbl
        o.id = len(self.ops)
        o.eng = eng
        o.fn = fn
        o.is_dma = dma is not None
        o.target = False
        o.val = None
        o.cost = (DEF_COST[eng] if cost is None else cost) + OVH.get(eng, 0.0)
        o.tag = None
        if TAGS:
            import sys as _s
            f = _s._getframe(1)
            while f.f_code.co_name in ("act", "tt", "stt", "<lambda>", "op"):
                f = f.f_back
            o.tag = "%s:%d" % (f.f_code.co_name, f.f_lineno)
        o.lat = lat
        if o.is_dma:
            if dma not in self.dma_sems:
                self.dma_sems[dma] = [self.n_dma_sem, 0]
                self.n_dma_sem += 1
            self.dma_sems[dma][1] += 16
            o.dsem = dma
            o.dval = self.dma_sems[dma][1]
            if cost is None:
                o.cost = 0.1 if eng == "sp" else 1.0
            if lat == 0.0:
                o.lat = 4.0
        preds = {}
        for r in reads:
            w = self.last_w.get(r)
            if w is not None:
                preds[w.id] = (w, True)
        for r in writes:
            w = self.last_w.get(r)
            if w is not None:
                ns = ((SYNC_ALL and eng != "pe") or w.is_dma or o.is_dma or w.eng != eng or eng == "pool")
                if w.id not in preds or ns:
                    preds[w.id] = (w, ns or preds.get(w.id, (w, False))[1])
            for rd in self.readers.get(r, ()):
                ns = ((SYNC_ALL and eng != "pe") or rd.is_dma or o.is_dma or rd.eng != eng or eng == "pool")
                if rd.id not in preds:
                    preds[rd.id] = (rd, ns)
                elif ns:
                    preds[rd.id] = (rd, True)
        preds.pop(o.id, None)
        o.preds = list(preds.values())
        self.eng_ops[eng].append(o)
        self.ops.append(o)
        for r in reads:
            self.readers.setdefault(r, []).append(o)
        for r in writes:
            self.last_w[r] = o
            self.readers[r] = []
        return o

    def merge(self, dst, srcs):
        lst = self.readers.setdefault(dst, [])
        for k in srcs:
            lst.extend(self.readers.get(k, ()))
            w = self.last_w.get(k)
            if w is not None:
                lst.append(w)

    def finalize(self, reorder=True, window=96):
        pend = {e: list(self.eng_ops[e]) for e in self.ENGS}
        bl = [0.0] * len(self.ops)
        for o in reversed(self.ops):
            v = bl[o.id] + o.cost + o.lat
            bl[o.id] = v
            for (p, _) in o.preds:
                if bl[p.id] < v:
                    bl[p.id] = v
        use_bl = PRIO_BL
        free = {e: 0.0 for e in self.ENGS}
        act_state = None
        n_switch = 0
        sched = []
        done = set()
        order = {e: [] for e in self.ENGS}
        total = len(self.ops)
        while len(sched) < total:
            best = None
            for e in self.ENGS:
                lst = pend[e]
                if not lst:
                    continue
                lim = window if reorder else 1
                cnt = 0
                for o in lst:
                    cnt += 1
                    if cnt > lim:
                        break
                    ok = True
                    t = free[e]
                    for (p, _) in o.preds:
                        if p.id not in done:
                            ok = False
                            break
                        if p.fin > t:
                            t = p.fin
                    if not ok:
                        continue
                    pen = 0.0
                    if e == "act" and o.tbl is not None and act_state is not None and o.tbl != act_state:
                        pen = TBL_PEN + (TBL_HOLD if act_state == "sqrt" else 0.0)
                    if use_bl:
                        key = (round(t / SLOT_) , -bl[o.id], o.id)
                    else:
                        key = (t + pen, o.id)
                    if best is None or key < best[0]:
                        best = (key, e, o, t)
                    if (not use_bl) and pen == 0.0 and t <= free[e]:
                        break
            assert best is not None, "scheduler deadlock"
            _, e, o, t = best
            o.start = t
            c_eff = o.cost
            if e == "act" and o.tbl is not None:
                if act_state is not None and o.tbl != act_state:
                    c_eff += 1.3
                    n_switch += 1
                act_state = o.tbl
            free[e] = t + c_eff
            o.fin = t + c_eff + o.lat
            pend[e].remove(o)
            done.add(o.id)
            sched.append(o)
            order[e].append(o)
        self.eng_ops = order
        self.makespan = max(o.fin for o in sched)
        self.n_switch = n_switch
        clock = {e: {} for e in self.ENGS}
        nseq = {e: 0 for e in self.ENGS}
        for o in sched:
            eng = o.eng
            clk = clock[eng]
            waits = []
            for (d, ns) in sorted(o.preds, key=lambda q: -(q[0].start + 1e-9 * q[0].id)):
                if not ns:
                    continue
                if d.is_dma:
                    known = clk.get(("d", d.id), 0) >= 1
                else:
                    known = clk.get(d.eng, 0) >= d.seq
                if known:
                    continue
                waits.append(d)
                d.target = True
                for k, v in d.clock.items():
                    if isinstance(k, tuple) and not (d.is_dma and k == ("d", d.id)):
                        continue
                    if clk.get(k, 0) < v:
                        clk[k] = v
            o.waits = waits
            nseq[eng] += 1
            if o.is_dma:
                o.seq = None
                o.clock = dict(clk)
                o.clock[("d", o.id)] = 1
            else:
                o.seq = nseq[eng]
                c2 = dict(clk)
                c2[eng] = o.seq
                o.clock = c2
        for e in self.ENGS:
            n = 0
            for o in self.eng_ops[e]:
                if (not o.is_dma) and o.target:
                    n += 1
                    o.val = n


def fsz(ap):
    n = 1
    for d in ap.shape[1:]:
        n *= d
    return n


def c_act(n):
    return (250 + n) / 1200.0


def c_dve(n):
    return (170 + n) / 960.0


def c_pool(n):
    return (200 + 2.9 * n) / 1200.0


class Pool_:
    def __init__(self, items):
        self.free = list(items)

    def get(self):
        assert self.free, "pool exhausted"
        return self.free.pop(0)

    def put(self, it):
        self.free.append(it)


def build(NL=DEPTH, n_tiles_dbg=None):
    nc = bass.Bass("TRN2", target_bir_lowering=False)
    x_d = nc.dram_tensor("x", [TTOT, D], F32, kind="ExternalInput").ap()
    win_d = nc.dram_tensor("w_in", [NL, D, 3072], F32, kind="ExternalInput").ap()
    wout_d = nc.dram_tensor("w_out", [NL, D, D], F32, kind="ExternalInput").ap()
    prm_d = nc.dram_tensor("prm", [NL, 128, 128], F32, kind="ExternalInput").ap()
    dvb_d = nc.dram_tensor("dvb", [NL, 3 * 256], F32, kind="ExternalInput").ap()
    sguw_d = nc.dram_tensor("sguwT", [NL, 4, 128, 128], F32, kind="ExternalInput").ap()
    sgub_d = nc.dram_tensor("sgub", [NL, 4, 128], F32, kind="ExternalInput").ap()
    poolw_d = nc.dram_tensor("pool_w", [NL, 4, 64, 64], F32, kind="ExternalInput").ap()
    ident_d = nc.dram_tensor("ident", [128, 128], F32, kind="ExternalInput").ap()
    invc_d = nc.dram_tensor("invc", [128, 32], F32, kind="ExternalInput").ap()
    y_d = nc.dram_tensor("y", [TMAIN, D], F32, kind="ExternalOutput").ap()

    trk = Trk()
    NTMP = 10
    from contextlib import ExitStack
    with ExitStack() as es:
        def sb(name, shape, dt):
            return es.enter_context(nc.sbuf_tensor(name, shape, dt))
        x32 = sb("x32", [128, NCH, TTOT], F32)
        win = sb("win", [128, NCH, 3072], BF16)
        wout = sb("wout", [128, NCH, D], BF16)
        dA = sb("dA", [128, 2 * KA, 128], BF16)
        dB = sb("dB", [128, 6, 128], BF16)
        PW = sb("PW", [128, 2, 128], BF16)
        WT = sb("WT", [128, 4, 128], BF16)
        Bavg = sb("Bavg", [128, 128], BF16)
        ones1k = sb("ones1k", [128, 128], BF16)
        ident = sb("ident_sb", [128, 128], F32)
        prmT = sb("prmT", [128, NL, 128], F32)
        dvb = sb("dvb_sb", [128, 3 * 256], F32)
        sgub = sb("sgub_sb", [128, 2, 128], F32)
        invc = sb("invc_sb", [128, 32], F32)
        epsc = sb("epsc", [128, 1], F32)
        xbfs = [sb("xbf%d" % i, [128, NCH, TW], BF16) for i in range(NXBF)]
        mixT = [sb("mixT%d" % i, [128, NCH, TW], BF16) for i in range(2)]
        abf0 = sb("abf0", [128, 2, 30 + TW], BF16)
        chb0 = sb("chb0", [128, 2, 2 + TW], BF16)
        hcb0 = sb("hcb0", [128, 2, 16 + TW], F32)
        pooled = sb("pooled", [128, 2, TW], BF16)
        st6 = sb("st6", [128, 2, 8], F32)
        mv = sb("mv", [128, 4, 2], F32)
        fx = sb("fx", [128, 16], F32)
        cst = sb("cst", [128, 4], F32)
        hb = sb("hb", [128, NL, 2], F32)
        tmps = [sb("tmp%d" % i, [128, 2, TW], F32) for i in range(NTMP)]
        banks = [es.enter_context(nc.psum_tensor("bank%d" % i, [128, 512], F32)) for i in range(8)]
        sems = {e: es.enter_context(nc.semaphore("sem_" + e)) for e in Trk.ENGS}
        dsems = [es.enter_context(nc.semaphore("dsem%d" % i)) for i in range(72)]

        tpool = Pool_(list(range(NTMP)))
        bpool = Pool_(list(range(8)))

        def TK(i):
            return ("tmp", i)

        def BK(i):
            return ("bank", i)

        trk.op("sp", lambda e: e.dma_start(out=ident[:], in_=ident_d[:, :]), writes=["ident"], dma="c0")
        trk.op("sp", lambda e: e.dma_start(out=invc[:], in_=invc_d[:, :]), writes=["invc"], dma="c1")
        trk.op("pool", lambda e: e.memset(Bavg[:], 0.0), writes=["Bavg"])
        trk.op("pool", lambda e: e.memset(Bavg[0:64, 0:64], 1.0 / 64), writes=["Bavg"])
        trk.op("pool", lambda e: e.memset(Bavg[64:128, 64:128], 1.0 / 64), writes=["Bavg"])
        trk.op("pool", lambda e: e.memset(ones1k[:], 1.0 / 1024), writes=["ones1k"])
        trk.op("pool", lambda e: e.memset(epsc[:], EPS), writes=["epsc"])
        trk.op("pool", lambda e: e.memset(cst[:], -0.5), writes=["cst"])
        for l in range(NL):
            t = tpool.get()
            trk.op("sp", lambda e, l=l, t=t: e.dma_start(out=tmps[t][:, 0, 0:128], in_=prm_d[l]),
                   writes=[TK(t)], dma="prm%d" % l)
            b = bpool.get()
            trk.op("pe", lambda e, t=t, b=b: e.transpose(banks[b][:, 0:128], tmps[t][:, 0, 0:128], ident[:]),
                   reads=[TK(t), "ident"], writes=[BK(b)])
            trk.op("dve", lambda e, l=l, b=b: e.tensor_copy(prmT[:, l, :], banks[b][:, 0:128]),
                   reads=[BK(b)], writes=[("prmT", l)])
            tpool.put(t)
            bpool.put(b)
            trk.op("dve", lambda e, l=l: e.tensor_scalar(hb[:, l, :], prmT[:, l, R_BIN + 2 * S_AGLU:R_BIN + 2 * S_AGLU + 2],
                                                        0.5, None, ALU.mult),
                   reads=[("prmT", l)], writes=[("hb", l)])

        def col(l, r):
            return prmT[:, l, r:r + 1]

        nblk = TTOT // 128
        stg = [mixT[0], xbfs[1], mixT[1], xbfs[2]]
        stgk = [("mixT", 0), ("xbf", 1), ("mixT", 1), ("xbf", 2)]
        STG_SEQ = [0, 1, 2, 3, 0, 1, 2, 3] + [2, 3] * 5
        stgv = [m_[:].rearrange("p a b -> p (a b)").bitcast(F32) for m_ in stg]
        for bI in range(nblk):
            s = STG_SEQ[bI]
            trk.op("sp", lambda e, bI=bI, s=s: e.dma_start(
                out=stgv[s], in_=x_d[bI * 128:(bI + 1) * 128, :]),
                writes=[stgk[s]], dma="xs%d" % s)
            for half in range(2):
                b = bpool.get()
                for q in range(4):
                    m = half * 4 + q
                    trk.op("pe", lambda e, s=s, m=m, q=q, b=b: e.transpose(
                        banks[b][:, q * 128:(q + 1) * 128],
                        stgv[s][:, m * 128:(m + 1) * 128], ident[:]),
                        reads=[stgk[s], "ident"], writes=[BK(b)])
                eng = "act" if half == 0 else "dve"
                if eng == "act":
                    trk.op("act", lambda e, b=b, half=half, bI=bI: e.activation(
                        out=x32[:, half * 4:(half + 1) * 4, bI * 128:(bI + 1) * 128],
                        in_=banks[b][:].rearrange("p (q k) -> p q k", k=128), func=AF.Identity),
                        reads=[BK(b)], writes=[("x", m, bI) for m in range(half * 4, half * 4 + 4)])
                else:
                    trk.op("dve", lambda e, b=b, half=half, bI=bI: e.tensor_copy(
                        x32[:, half * 4:(half + 1) * 4, bI * 128:(bI + 1) * 128],
                        banks[b][:].rearrange("p (q k) -> p q k", k=128)),
                        reads=[BK(b)], writes=[("x", m, bI) for m in range(half * 4, half * 4 + 4)])
                bpool.put(b)

        win_src = [win_d[l].rearrange("(kc p) e -> p kc e", p=128) for l in range(NL)]
        wout_src = [wout_d[l].rearrange("(kc p) e -> p kc e", p=128) for l in range(NL)]

        def load_win_slice(l, s):
            trk.op("pool", lambda e: e.dma_start(out=win[:, :, s * 256:(s + 1) * 256],
                                                 in_=win_src[l][:, :, s * 256:(s + 1) * 256]),
                   writes=[("win", s)], dma="win%d" % s, lat=9.0)

        def load_wout(l):
            for kc in range(NCH):
                trk.op("pool", lambda e, kc=kc: e.dma_start(out=wout[:, kc, :], in_=wout_src[l][:, kc, :]),
                       writes=[("wout", kc)], dma="wout%d" % kc)

        def load_layer_consts(l):
            for k in range(KA):
                for c in range(2):
                    i = 2 * k + c
                    if i % 2 == 0:
                        trk.op("act", lambda e, i=i: e.activation(out=dA[:, i, :], in_=ident[:], func=AF.Identity,
                                                                 scale=col(l, R_CAW + i)),
                               reads=["ident", ("prmT", l)], writes=[("dA", i)])
                    else:
                        trk.op("dve", lambda e, i=i: e.tensor_scalar(dA[:, i, :], ident[:], col(l, R_CAW + i), None,
                                                                    ALU.mult),
                               reads=["ident", ("prmT", l)], writes=[("dA", i)])
            for i in range(6):
                trk.op("dve", lambda e, i=i: e.tensor_scalar(dB[:, i, :], ident[:], col(l, R_CBW + i), None, ALU.mult),
                       reads=["ident", ("prmT", l)], writes=[("dB", i)])
            trk.op("pool", lambda e: e.memset(PW[:], 0.0), writes=["PW"])
            for g in range(4):
                c, h = g // 2, g % 2
                trk.op("pool", lambda e, g=g, c=c, h=h: e.dma_start(
                    out=PW[h * 64:(h + 1) * 64, c, h * 64:(h + 1) * 64], in_=poolw_d[l, g]),
                    writes=["PW"], dma="pw")
            for h in range(4):
                trk.op("pool", lambda e, h=h: e.dma_start(out=WT[:, h, :], in_=sguw_d[l, h]),
                       writes=["WT"], dma="wt")
            trk.op("pool", lambda e: e.memset(WT[64:128, :, 0:64], 0.0), writes=["WT"])
            trk.op("sp", lambda e: e.dma_start(out=dvb[:], in_=dvb_d[l:l + 1, :].partition_broadcast(128)),
                   writes=["dvb"], dma="dvb")
            for h in range(4):
                c, hh = h // 2, h % 2
                trk.op("sp", lambda e, h=h, c=c, hh=hh: e.dma_start(
                    out=sgub[hh * 64:(hh + 1) * 64, c, :], in_=sgub_d[l, h:h + 1, :].partition_broadcast(64)),
                    writes=["sgub"], dma="sgub")

        def bias(l, s, c):
            return col(l, R_BIN + 2 * s + c)

        def inproj(l, s, w, b, xi):
            for c in range(2):
                for kc in range(NCH):
                    trk.op("pe", lambda e, c=c, kc=kc: e.matmul(
                        banks[b][:, c * 256:c * 256 + w],
                        win[:, kc, s * 256 + c * 128:s * 256 + (c + 1) * 128],
                        xbfs[xi][:, kc, 0:w], start=(kc == 0), stop=(kc == NCH - 1)),
                        reads=[("win", s), ("xbf", xi)], writes=[BK(b)])

        def bv(b, w):
            return banks[b][:].rearrange("p (c t) -> p c t", c=2)[:, :, 0:w]

        def tv(t, w):
            return tmps[t][:, :, 0:w]

        def flat(t):
            return tmps[t][:].rearrange("p a b -> p (a b)")

        def bfv(t):
            return flat(t).bitcast(BF16)

        class GS:
            pass

        def gen_tile(st, l, t0, w, par, kind, first_in_layer, last_layer, uid):
            nb = w // 128
            blks = [t0 // 128 + j for j in range(nb)]
            xkeys = [("x", m, bI) for m in range(NCH) for bI in blks]
            A = abf0; C_ = chb0; H = hcb0
            xi = uid % NXBF
            KAH, KAC, KCH, KCC, KHH, KHC = "abfH", "abfC", "chbH", "chbC", "hcbH", "hcbC"
            mx = mixT[par]
            mk = ("mixT", par)
            xk_ = ("xbf", uid % NXBF)
            entering_main = (t0 + w == HALO)

            def act(out, in_, func, bias_=None, scale=None, rd=(), wr=()):
                kw = {}
                if bias_ is not None:
                    kw["bias"] = bias_
                if scale is not None:
                    kw["scale"] = scale
                cst_ = c_act(fsz(out))
                tbl_ = "sqrt" if func == AF.Sqrt else ("silu" if func in (AF.Silu, AF.Tanh) else None)
                trk.op("act", lambda e: e.activation(out=out, in_=in_, func=func, **kw), reads=list(rd), writes=list(wr), cost=cst_, tbl=tbl_)

            def tt(eng, out, a, b_, op, rd, wr):
                n_ = fsz(out)
                cst_ = c_dve(n_) if eng == "dve" else (0.17 * n_ + 0.3 if op == ALU.pow else c_pool(n_))
                trk.op(eng, lambda e: e.tensor_tensor(out, a, b_, op), reads=list(rd), writes=list(wr), cost=cst_)

            def stt(eng, out, a, sc, b_, op0, op1, rd, wr):
                n_ = fsz(out)
                cst_ = c_dve(n_) if eng == "dve" else c_pool(n_)
                trk.op(eng, lambda e: e.scalar_tensor_tensor(out, a, sc, b_, op0, op1), reads=list(rd), writes=list(wr), cost=cst_)

            P_ = ("prmT", l)
            if first_in_layer:
                trk.op("pool", lambda e: e.memset(A[:, :, 0:30], 0.0), writes=[KAH])
                trk.op("pool", lambda e: e.memset(C_[:, :, 0:2], 0.0), writes=[KCH])
                trk.op("pool", lambda e: e.memset(H[:, :, 0:16], 0.0), writes=[KHH])
            if CAST_POOL:
                trk.op("pool", lambda e: e.tensor_copy(xbfs[xi][:, :, 0:w], x32[:, :, t0:t0 + w]),
                       reads=xkeys, writes=[xk_], cost=0.0038 * 8 * w)
            else:
                act(xbfs[xi][:, :, 0:w], x32[:, :, t0:t0 + w], AF.Identity, rd=xkeys, wr=[xk_])
            yield
            b_glu = bpool.get(); inproj(l, S_AGLU, w, b_glu, xi)
            b_val = bpool.get(); inproj(l, S_AVAL, w, b_val, xi)
            t_sig = tpool.get()
            for c in range(2):
                act(tmps[t_sig][:, c, 0:w], banks[b_glu][:, c * 256:c * 256 + w], AF.Tanh, bias_=hb[:, l, c:c + 1], scale=0.5,
                    rd=[BK(b_glu), ("hb", l)], wr=[TK(t_sig)])
            bpool.put(b_glu)
            trk.op("dve", lambda e: e.tensor_scalar(tv(t_sig, w), tv(t_sig, w), 0.5, 0.5, ALU.mult, ALU.add),
                   reads=[TK(t_sig)], writes=[TK(t_sig)])
            for c in range(2):
                stt("dve", A[:, c, 30:30 + w], banks[b_val][:, c * 256:c * 256 + w], bias(l, S_AVAL, c), tmps[t_sig][:, c, 0:w],
                    ALU.add, ALU.mult, [BK(b_val), TK(t_sig), P_], [KAC])
            bpool.put(b_val); tpool.put(t_sig)
            yield
            b_bc = bpool.get(); inproj(l, S_BC, w, b_bc, xi)
            b_bh = bpool.get(); inproj(l, S_BH, w, b_bh, xi)
            t_bc = tpool.get()
            for c in range(2):
                act(tmps[t_bc][:, c, 0:w], banks[b_bc][:, c * 256:c * 256 + w], AF.Identity, bias_=bias(l, S_BC, c), scale=1.0,
                    rd=[BK(b_bc), P_], wr=[TK(t_bc)])
            bpool.put(b_bc)
            for c in range(2):
                stt("dve", C_[:, c, 2:2 + w], banks[b_bh][:, c * 256:c * 256 + w], bias(l, S_BH, c), tmps[t_bc][:, c, 0:w],
                    ALU.add, ALU.mult, [BK(b_bh), TK(t_bc), P_], [KCC])
            bpool.put(b_bh); tpool.put(t_bc)
            yield
            b_ch = bpool.get(); inproj(l, S_CH, w, b_ch, xi)
            if kind == "full":
                b_az = bpool.get(); inproj(l, S_AZ, w, b_az, xi)
            for c in range(2):
                act(H[:, c, 16:16 + w], banks[b_ch][:, c * 256:c * 256 + w], AF.Identity, bias_=bias(l, S_CH, c), scale=1.0,
                    rd=[BK(b_ch), P_], wr=[KHC])
            bpool.put(b_ch)

            def hand(dst, src, key_d, key_s):
                if entering_main:
                    act(dst, src, AF.Identity, scale=col(l, R_FLAG), rd=[key_s, P_], wr=[key_d])
                else:
                    trk.op("pool", lambda e: e.tensor_copy(dst, src), reads=[key_s], writes=[key_d])
            if kind == "hist":
                hand(A[:, :, 0:30], A[:, :, w:w + 30], KAH, KAC)
                hand(C_[:, :, 0:2], C_[:, :, w:w + 2], KCH, KCC)
                hand(H[:, :, 0:16], H[:, :, w:w + 16], KHH, KHC)
            if kind == "hist":
                st.win_done = True
                st.consts_done = True
                st.wout_done = True
                return
            t_sza = tpool.get()
            for c in range(2):
                act(tmps[t_sza][:, c, 0:w], banks[b_az][:, c * 256:c * 256 + w], AF.Silu, bias_=bias(l, S_AZ, c), scale=1.0,
                    rd=[BK(b_az), P_], wr=[TK(t_sza)])
            bpool.put(b_az)
            yield
            b_cv = bpool.get()
            for c in range(2):
                for k in range(KA):
                    trk.op("pe", lambda e, c=c, k=k: e.matmul(
                        banks[b_cv][:, c * 256:c * 256 + w], dA[:, 2 * k + c, :], A[:, c, k:k + w],
                        start=(k == 0), stop=(k == KA - 1)),
                        reads=[("dA", 2 * k + c), KAH, KAC], writes=[BK(b_cv)])
            hand(A[:, :, 0:30], A[:, :, w:w + 30], KAH, KAC)
            b_bb = bpool.get(); inproj(l, S_BB, w, b_bb, xi)
            t_c = tpool.get(); t_bq = tpool.get()
            trk.merge(("cb", uid), [TK(t_bq)])
            trk.merge(("csq", uid), [TK(t_bq)])
            cbv = bfv(t_bq)

            def cb_ap(c):
                return cbv[:, c * 256:c * 256 + w]

            def csq_ap(c):
                return cbv[:, 512 + c * 256:512 + c * 256 + w]
            for c in range(2):
                act(tmps[t_c][:, c, 0:w], banks[b_cv][:, c * 256:c * 256 + w], AF.Identity, bias_=col(l, R_CAB + c), scale=1.0,
                    rd=[BK(b_cv), P_], wr=[TK(t_c)])
                act(csq_ap(c), banks[b_cv][:, c * 256:c * 256 + w], AF.Square, bias_=col(l, R_CAB + c), scale=1.0,
                    rd=[BK(b_cv), P_], wr=[("csq", uid)])
                act(cb_ap(c), banks[b_cv][:, c * 256:c * 256 + w], AF.Identity, bias_=col(l, R_CAB + c), scale=1.0,
                    rd=[BK(b_cv), P_], wr=[("cb", uid)])
            bpool.put(b_cv)
            yield
            b_bz = bpool.get(); inproj(l, S_BZ, w, b_bz, xi)
            b_cb = bpool.get()
            for c in range(2):
                for k in range(3):
                    trk.op("pe", lambda e, c=c, k=k: e.matmul(
                        banks[b_cb][:, c * 256:c * 256 + w], dB[:, 2 * k + c, :], C_[:, c, k:k + w],
                        start=(k == 0), stop=(k == 2)),
                        reads=[("dB", 2 * k + c), KCH, KCC], writes=[BK(b_cb)])
            hand(C_[:, :, 0:2], C_[:, :, w:w + 2], KCH, KCC)
            t_szb = tpool.get()
            for c in range(2):
                act(tmps[t_szb][:, c, 0:w], banks[b_bz][:, c * 256:c * 256 + w], AF.Silu, bias_=bias(l, S_BZ, c), scale=1.0,
                    rd=[BK(b_bz), P_], wr=[TK(t_szb)])
            bpool.put(b_bz)
            for c in range(2):
                stt("dve", tmps[t_szb][:, c, 0:w], banks[b_bb][:, c * 256:c * 256 + w], bias(l, S_BB, c), tmps[t_szb][:, c, 0:w],
                    ALU.add, ALU.mult, [BK(b_bb), TK(t_szb), P_], [TK(t_szb)])
            bpool.put(b_bb)
            tt("dve", mx[:, 2:4, 0:w], tv(t_szb, w), bv(b_cb, w), ALU.mult, [TK(t_szb), BK(b_cb)], [mk])
            bpool.put(b_cb); tpool.put(t_szb)
            hk = "hcbHC"
            W_ = 16 + w
            t_pa = tpool.get(); t_pb = tpool.get()
            pa = flat(t_pa); pb = flat(t_pb)
            ka, kb = TK(t_pa), TK(t_pb)
            for c in range(2):
                Hc = H[:, c, :]
                tt(POOL_ENG, pa[:, 1:W_], Hc[:, 1:W_], Hc[:, 0:W_ - 1], ALU.add, [KHH, KHC], [ka])
                tt(POOL_ENG, pb[:, 3:W_], pa[:, 3:W_], pa[:, 1:W_ - 2], ALU.add, [ka], [kb])
                if c == 1:
                    tt(POOL_ENG, pa[:, 7:W_], pb[:, 7:W_], pb[:, 3:W_ - 4], ALU.add, [kb], [ka])
                    tt(POOL_ENG, pb[:, 15:W_], pa[:, 15:W_], pa[:, 7:W_ - 8], ALU.add, [ka], [kb])
                pinv = col(l, R_PINV + c)
                stt("dve", pooled[0:64, c, 0:w], pa[0:64, 16:W_], pinv[0:64], H[0:64, c, 16:W_], ALU.mult, ALU.subtract,
                    [ka, KHH, KHC, P_], ["pooled"])
                stt("dve", pooled[64:128, c, 0:w], pb[64:128, 16:W_], pinv[64:128], H[64:128, c, 16:W_], ALU.mult, ALU.subtract,
                    [kb, KHH, KHC, P_], ["pooled"])
                if t0 == HALO:
                    for (src, sk, p0) in ((pa, ka, 0), (pb, kb, 64)):
                        tt("dve", fx[p0:p0 + 64, 0:16], src[p0:p0 + 64, 16:32], invc[p0:p0 + 64, c * 16:(c + 1) * 16],
                           ALU.mult, [sk, "invc"], ["fx"])
                        tt("dve", pooled[p0:p0 + 64, c, 0:16], fx[p0:p0 + 64, 0:16], H[p0:p0 + 64, c, 16:32],
                           ALU.subtract, ["fx", KHH, KHC], ["pooled"])
            tpool.put(t_pa); tpool.put(t_pb)
            hand(H[:, :, 0:16], H[:, :, w:w + 16], KHH, KHC)
            yield
            b_cz = bpool.get(); inproj(l, S_CZ, w, b_cz, xi)
            b_dv = bpool.get()
            for j in range(nb):
                for kc in range(NCH):
                    trk.op("pe", lambda e, j=j, kc=kc: e.matmul(
                        banks[b_dv][:, j * 256:(j + 1) * 256], xbfs[xi][:, kc, j * 128:(j + 1) * 128],
                        win[:, kc, S_DV * 256:(S_DV + 1) * 256], start=(kc == 0), stop=(kc == NCH - 1)),
                        reads=[("win", S_DV), xk_], writes=[BK(b_dv)])
            b_mean = bpool.get(); b_ex2 = bpool.get()
            for c in range(2):
                trk.op("pe", lambda e, c=c: e.matmul(banks[b_mean][:, c * 256:c * 256 + w], Bavg[:], cb_ap(c),
                                                     start=True, stop=True),
                       reads=["Bavg", ("cb", uid)], writes=[BK(b_mean)])
            for c in range(2):
                trk.op("pe", lambda e, c=c: e.matmul(banks[b_ex2][:, c * 256:c * 256 + w], Bavg[:], csq_ap(c),
                                                     start=True, stop=True),
                       reads=["Bavg", ("csq", uid)], writes=[BK(b_ex2)])
            trk.merge(TK(t_bq), [("cb", uid), ("csq", uid)])
            tpool.put(t_bq)
            t_szc = tpool.get()
            for c in range(2):
                act(tmps[t_szc][:, c, 0:w], banks[b_cz][:, c * 256:c * 256 + w], AF.Silu, bias_=bias(l, S_CZ, c), scale=1.0,
                    rd=[BK(b_cz), P_], wr=[TK(t_szc)])
            bpool.put(b_cz)
            t_v = tpool.get(); t_vb = tpool.get()
            vbv = bfv(t_vb)
            db3 = dvb[:, 0:256].unsqueeze(1)
            if nb > 1:
                db3 = db3.to_broadcast([128, nb, 256])
            tt("dve", tmps[t_v][:, 0:nb, :], banks[b_dv][:].rearrange("p (j c) -> p j c", j=2)[:, 0:nb, :], db3, ALU.add,
               [BK(b_dv), "dvb"], [TK(t_v)])
            bpool.put(b_dv)
            for j in range(nb):
                trk.op("dve", lambda e, j=j: e.bn_stats(st6[:, j, 0:6], tmps[t_v][:, j, :]), reads=[TK(t_v)], writes=[("st6", j)])
                trk.op("dve", lambda e, j=j: e.bn_aggr(mv[:, 0:2, j], st6[:, j, 0:6]), reads=[("st6", j)], writes=["mv01"])
            trk.op("dve", lambda e: e.tensor_scalar(mv[:, 2, 0:nb], mv[:, 1, 0:nb], EPS, None, ALU.add),
                   reads=["mv01"], writes=["mv2"])
            tt("pool", mv[:, 2, 0:nb], mv[:, 2, 0:nb], cst[:, 2:2 + nb], ALU.pow, ["mv2", "cst"], ["mv2"])
            for j in range(nb):
                stt("dve", tmps[t_v][:, j, :], tmps[t_v][:, j, :], mv[:, 0, j:j + 1], dvb[:, 256:512], ALU.subtract, ALU.mult,
                    [TK(t_v), "mv01", "dvb"], [TK(t_v)])
            for j in range(nb):
                stt("dve", vbv[:, j * 256:(j + 1) * 256], tmps[t_v][:, j, :], mv[:, 2, j:j + 1], dvb[:, 512:768], ALU.mult, ALU.add,
                    [TK(t_v), "mv2", "dvb"], [TK(t_vb)])
            tpool.put(t_v)
            t_ra = tpool.get()
            act(tv(t_ra, w), bv(b_mean, w), AF.Square, rd=[BK(b_mean)], wr=[TK(t_ra)])
            stt("dve", tv(t_ra, w), bv(b_ex2, w), EPS, tv(t_ra, w), ALU.add, ALU.subtract, [BK(b_ex2), TK(t_ra)], [TK(t_ra)])
            bpool.put(b_ex2)
            act(tv(t_ra, w), tv(t_ra, w), AF.Sqrt, rd=[TK(t_ra)], wr=[TK(t_ra)])
            trk.op("dve", lambda e: e.reciprocal(tv(t_ra, w), tv(t_ra, w)), reads=[TK(t_ra)], writes=[TK(t_ra)], cost=0.0066 * 2 * w + 0.1)
            tt("dve", tv(t_c, w), tv(t_c, w), bv(b_mean, w), ALU.subtract, [TK(t_c), BK(b_mean)], [TK(t_c)])
            bpool.put(b_mean)
            yield
            b_du = bpool.get(); inproj(l, S_DU, w, b_du, xi)
            b_dz = bpool.get(); inproj(l, S_DZ, w, b_dz, xi)
            st.win_done = True
            t_du = tpool.get(); t_szd = tpool.get()
            for c in range(2):
                act(tmps[t_du][:, c, 0:w], banks[b_du][:, c * 256:c * 256 + w], AF.Identity, bias_=bias(l, S_DU, c), scale=1.0,
                    rd=[BK(b_du), P_], wr=[TK(t_du)])
                act(tmps[t_szd][:, c, 0:w], banks[b_dz][:, c * 256:c * 256 + w], AF.Silu, bias_=bias(l, S_DZ, c), scale=1.0,
                    rd=[BK(b_dz), P_], wr=[TK(t_szd)])
            bpool.put(b_du); bpool.put(b_dz)
            tt("pool", tv(t_c, w), tv(t_c, w), tv(t_ra, w), ALU.mult, [TK(t_c), TK(t_ra)], [TK(t_c)])
            tpool.put(t_ra)
            for c in range(2):
                act(tmps[t_c][:, c, 0:w], tmps[t_c][:, c, 0:w], AF.Silu, bias_=col(l, R_NAB + c), scale=col(l, R_NAG + c),
                    rd=[TK(t_c), P_], wr=[TK(t_c)])
            tt(YA_ENG, mx[:, 0:2, 0:w], tv(t_c, w), tv(t_sza, w), ALU.mult, [TK(t_c), TK(t_sza)], [mk])
            tpool.put(t_c); tpool.put(t_sza)
            tt("dve", tv(t_du, w), tv(t_du, w), tv(t_szd, w), ALU.mult, [TK(t_du), TK(t_szd)], [TK(t_du)])
            tpool.put(t_szd)
            yield
            b_pw = bpool.get()
            for c in range(2):
                trk.op("pe", lambda e, c=c: e.matmul(banks[b_pw][:, c * 256:c * 256 + w], PW[:, c, :], pooled[:, c, 0:w],
                                                     start=True, stop=True),
                       reads=["PW", "pooled"], writes=[BK(b_pw)])
            for c in range(2):
                stt("dve", mx[:, 4 + c, 0:w], banks[b_pw][:, c * 256:c * 256 + w], col(l, R_PSC + c), tmps[t_szc][:, c, 0:w],
                    ALU.mult, ALU.mult, [BK(b_pw), TK(t_szc), P_], [mk])
            bpool.put(b_pw); tpool.put(t_szc)
            yield
            b_sp = bpool.get()
            for j in range(nb):
                for h in range(4):
                    c, hh = h // 2, h % 2
                    trk.op("pe", lambda e, j=j, h=h, c=c, hh=hh: e.matmul(
                        banks[b_sp][hh * 64:(hh + 1) * 64, c * 256 + j * 128:c * 256 + (j + 1) * 128],
                        vbv[:, j * 256 + h * 64:j * 256 + (h + 1) * 64], WT[:, h, :], start=True, stop=True),
                        reads=[TK(t_vb), "WT"], writes=[BK(b_sp)])
            st.consts_done = True
            tpool.put(t_vb)
            t_sp = tpool.get()
            sg4 = sgub[:].unsqueeze(2)
            if nb > 1:
                sg4 = sg4.to_broadcast([128, 2, nb, 128])
            tt("dve", tv(t_sp, w).rearrange("p c (j i) -> p c j i", i=128),
               bv(b_sp, w).rearrange("p c (j i) -> p c j i", i=128), sg4, ALU.add,
               [BK(b_sp), "sgub"], [TK(t_sp)])
            bpool.put(b_sp)
            tt(YD_ENG, mx[:, 6:8, 0:w], tv(t_sp, w), tv(t_du, w), ALU.mult, [TK(t_sp), TK(t_du)], [mk])
            tpool.put(t_sp); tpool.put(t_du)
            yield
            yield
            assert wout_loaded[0] == l, "w_out for layer %d not loaded yet" % l
            b_mu = bpool.get()
            t_zb = tpool.get(); t_zq = tpool.get()
            zbv = bfv(t_zb); zqv = bfv(t_zq)
            for i_ in range(4):
                trk.merge(("zb", uid, i_), [TK(t_zb)])
                trk.merge(("zq", uid, i_), [TK(t_zq)])

            def zb_ap(m):
                return zbv[:, (m % 4) * 256:(m % 4) * 256 + w]

            def zq_ap(m):
                return zqv[:, (m % 4) * 256:(m % 4) * 256 + w]

            def stats(m):
                trk.op("pe", lambda e: e.matmul(banks[b_mu][:, 0:w], ones1k[:], zb_ap(m), start=(m == 0), stop=(m == NCH - 1),
                                                skip_group_check=True),
                       reads=["ones1k", ("zb", uid, m % 4)], writes=[BK(b_mu)])
                trk.op("pe", lambda e: e.matmul(banks[b_mu][:, 256:256 + w], ones1k[:], zq_ap(m), start=False, stop=(m == NCH - 1),
                                                skip_group_check=True),
                       reads=["ones1k", ("zq", uid, m % 4)], writes=[BK(b_mu)])
            for mp in range(4):
                b_o = bpool.get()
                for mm in range(2):
                    m = 2 * mp + mm
                    for kc in range(NCH):
                        trk.op("pe", lambda e, m=m, mm=mm, kc=kc, b_o=b_o: e.matmul(
                            banks[b_o][:, mm * 256:mm * 256 + w], wout[:, kc, m * 128:(m + 1) * 128], mx[:, kc, 0:w],
                            start=(kc == 0), stop=(kc == NCH - 1)),
                            reads=[("wout", kc), mk], writes=[BK(b_o)])
                if mp == 3:
                    st.wout_done = True
                if mp >= 2:
                    for m_ in (2 * mp - 4, 2 * mp - 3):
                        stats(m_)
                t_z = tpool.get()
                for mm in range(2):
                    m = 2 * mp + mm
                    act(tmps[t_z][:, mm, 0:w], banks[b_o][:, mm * 256:mm * 256 + w], AF.Identity, bias_=col(l, R_BOUT + m), scale=1.0,
                        rd=[BK(b_o), P_], wr=[TK(t_z)])
                bpool.put(b_o)
                xk2 = [("x", m_, bI) for m_ in (2 * mp, 2 * mp + 1) for bI in blks]
                xs2 = x32[:, 2 * mp:2 * mp + 2, t0:t0 + w]
                s0 = (2 * mp) % 4
                zq2 = zqv[:, s0 * 256:(s0 + 2) * 256].rearrange("p (a b) -> p a b", a=2)[:, :, 0:w]
                zb2 = zbv[:, s0 * 256:(s0 + 2) * 256].rearrange("p (a b) -> p a b", a=2)[:, :, 0:w]
                stt("dve", xs2, xs2, ALPHA, tmps[t_z][:, 0:2, 0:w], ALU.mult, ALU.add, xk2 + [TK(t_z)], xk2)
                tt("dve", zq2, xs2, xs2, ALU.mult, xk2, [("zq", uid, s0), ("zq", uid, s0 + 1)])
                if ZB_ACT:
                    act(zb2, xs2, AF.Identity, rd=xk2, wr=[("zb", uid, s0), ("zb", uid, s0 + 1)])
                else:
                    trk.op("dve", lambda e, zb2=zb2, xs2=xs2: e.tensor_copy(zb2, xs2),
                           reads=xk2, writes=[("zb", uid, s0), ("zb", uid, s0 + 1)], cost=0.35)
                tpool.put(t_z)
                yield
            stats(4); stats(5)
            yield
            stats(6); stats(7)
            trk.merge(TK(t_zb), [("zb", uid, i) for i in range(4)])
            trk.merge(TK(t_zq), [("zq", uid, i) for i in range(4)])
            tpool.put(t_zb); tpool.put(t_zq)
            t_r = tpool.get()
            R = tmps[t_r]
            act(R[:, 0, 0:w], banks[b_mu][:, 0:w], AF.Square, rd=[BK(b_mu)], wr=[TK(t_r)])
            stt("dve", R[:, 0, 0:w], banks[b_mu][:, 256:256 + w], EPS, R[:, 0, 0:w], ALU.add, ALU.subtract, [BK(b_mu), TK(t_r)], [TK(t_r)])
            act(R[:, 0, 0:w], R[:, 0, 0:w], AF.Sqrt, rd=[TK(t_r)], wr=[TK(t_r)])
            trk.op("dve", lambda e: e.reciprocal(R[:, 0, 0:w], R[:, 0, 0:w]), reads=[TK(t_r)], writes=[TK(t_r)], cost=0.0066 * w + 0.1)
            stt("dve", R[:, 1, 0:w], banks[b_mu][:, 0:w], -1.0, R[:, 0, 0:w], ALU.mult, ALU.mult, [BK(b_mu), TK(t_r)], [TK(t_r)])
            bpool.put(b_mu)
            yield
            for (eng, m0) in ((("dve", 0), (APPLY_HI_ENG, 4))):
                xk4 = [("x", m_, bI) for m_ in range(m0, m0 + 4) for bI in blks]
                xs4 = x32[:, m0:m0 + 4, t0:t0 + w]
                tt(eng, xs4, xs4, R[:, 0:1, 0:w].to_broadcast([128, 4, w]), ALU.mult, xk4 + [TK(t_r)], xk4)
                tt(eng, xs4, xs4, R[:, 1:2, 0:w].to_broadcast([128, 4, w]), ALU.add, xk4 + [TK(t_r)], xk4)
                for m in range(m0, m0 + 4):
                    xk = [("x", m, bI) for bI in blks]
                    act(x32[:, m, t0:t0 + w], x32[:, m, t0:t0 + w], AF.Identity, bias_=col(l, R_LNB + m), scale=col(l, R_LNG + m),
                        rd=xk + [P_], wr=xk)
                yield
            tpool.put(t_r)
            if last_layer:
                for j in range(nb):
                    bI = blks[j]
                    touts = [tpool.get(), tpool.get()]
                    for half in range(2):
                        b = bpool.get()
                        for q in range(4):
                            m = half * 4 + q
                            trk.op("pe", lambda e, m=m, q=q, b=b, bI=bI: e.transpose(
                                banks[b][:, q * 128:(q + 1) * 128], x32[:, m, bI * 128:(bI + 1) * 128], ident[:]),
                                reads=[("x", m, bI), "ident"], writes=[BK(b)])
                        tki = touts[half]
                        dst = flat(tki)
                        if half == 0:
                            act(dst, banks[b][:], AF.Identity, rd=[BK(b)], wr=[TK(tki)])
                        else:
                            trk.op("dve", lambda e, b=b, dst=dst: e.tensor_copy(dst, banks[b][:]),
                                   reads=[BK(b)], writes=[TK(tki)])
                        bpool.put(b)
                        r0 = bI * 128 - HALO
                        trk.op("sp", lambda e, dst=dst, r0=r0, half=half: e.dma_start(
                            out=y_d[r0:r0 + 128, half * 512:(half + 1) * 512], in_=dst),
                            reads=[TK(tki)], writes=[("y", bI, half)], dma="y%d_%d" % (bI, half))
                    tpool.put(touts[0]); tpool.put(touts[1])
                    yield

        starts_all = [("full", 0, None), ("hist", 0, 128), ("full", 128, None), ("hist", 128, 256)]
        sched = starts_all[4 - NL:]
        tiles = []
        for l in range(NL):
            kind, s0, full_from = sched[l]
            if kind == "hist":
                tiles.append((l, s0, 128, "hist"))
                pos = full_from
            else:
                pos = s0
            while pos < TTOT:
                w = 128 if pos % 256 == 128 else 256
                tiles.append((l, pos, w, "full"))
                pos += w
        if n_tiles_dbg is not None:
            tiles = tiles[:n_tiles_dbg]

        WIN_ORDER = (S_AGLU, S_AVAL, S_BC, S_BH, S_CH, S_AZ, S_BB, S_BZ, S_CZ, S_DV, S_DU, S_DZ)
        wout_loaded = [0]
        for s in WIN_ORDER:
            load_win_slice(0, s)
        load_layer_consts(0)
        load_wout(0)

        active = []
        LAG = LAG_

        def step(gs):
            try:
                next(gs.gen)
                gs.stage += 1
                return True
            except StopIteration:
                return False

        def step_all():
            for gs in list(active):
                if not step(gs):
                    active.remove(gs)
            maybe_wout()

        pend_wout = [None]

        def maybe_wout():
            l = pend_wout[0]
            if l is None:
                return
            if all(g.wout_done for g in active if g.l < l):
                load_wout(l)
                wout_loaded[0] = l
                pend_wout[0] = None

        par = 0
        for ti, (l, t0, w, kind) in enumerate(tiles):
            first = (ti == 0) or (tiles[ti - 1][0] != l)
            if first and l > 0:
                while any((not g.consts_done) for g in active if g.l < l):
                    step_all()
                for s in WIN_ORDER:
                    load_win_slice(l, s)
                load_layer_consts(l)
                pend_wout[0] = l
                maybe_wout()
            while len(active) >= NACTIVE_ or (active and active[-1].stage < LAG):
                step_all()
            gs = GS()
            gs.l = l
            gs.stage = 0
            gs.win_done = False
            gs.consts_done = False
            gs.wout_done = False
            gs.gen = gen_tile(gs, l, t0, w, par, kind, first, l == NL - 1, ti)
            active.append(gs)
            par = 1 - par
        while active:
            step_all()
        ykeys = [k for k in trk.last_w if isinstance(k, tuple) and k[0] == "y"]
        trk.op("sp", lambda e: e.nop(), reads=ykeys)

        trk.finalize(reorder=REORDER, window=WINDOW_)
        with nc.Block() as block:
            engs = {}

            def mk_runner(name):
                def run(e):
                    engs[name] = e
                return run
            order = {"pe": "tensor", "act": "scalar", "dve": "vector", "pool": "gpsimd", "sp": "sync"}
            def emit_engine(name):
                def f(e):
                    for o in trk.eng_ops[name]:
                        for d in o.waits:
                            if d.is_dma:
                                e.wait_ge(dsems[trk.dma_sems[d.dsem][0]], d.dval)
                            else:
                                e.wait_ge(sems[d.eng], d.val)
                        ins = o.fn(e)
                        if o.is_dma:
                            ins.then_inc(dsems[trk.dma_sems[o.dsem][0]], 16)
                        elif o.target:
                            ins.then_inc(sems[name], 1)
                return f
            block.tensor(emit_engine("pe"))
            block.scalar(emit_engine("act"))
            block.vector(emit_engine("dve"))
            block.gpsimd(emit_engine("pool"))
            block.sync(emit_engine("sp"))
    nc._trk_stats = {e: len(trk.eng_ops[e]) for e in Trk.ENGS}
    nc._trk_stats['makespan'] = trk.makespan
    nc._trk_stats['act_switches'] = trk.n_switch
    nc._trk = trk
    return nc


def host_prep(inputs, NL=DEPTH):
    g = {k: np.asarray(v, dtype=np.float32) for k, v in inputs.items()}
    x = g["x"]
    ident = np.eye(128, dtype=np.float32)
    wins = (2, 4, 8, 16)
    pinv = np.zeros((2, 128), np.float32)
    for c in range(2):
        for h in range(2):
            pinv[c, h * 64:(h + 1) * 64] = 1.0 / wins[2 * c + h]
    in_maps = []
    Ls = list(range(DEPTH))[:NL]
    w_in = np.ascontiguousarray(g["w_in"][Ls])
    w_out = np.ascontiguousarray(g["w_out"][Ls])
    dvb = np.stack([np.concatenate([g["b_in"][l][S_DV * 256:(S_DV + 1) * 256], g["sgu_ln_g"][l], g["sgu_ln_b"][l]])
                    for l in Ls]).astype(np.float32)
    sguwT = np.ascontiguousarray(np.transpose(g["sgu_w"][Ls], (0, 1, 3, 2)))
    sgub = np.ascontiguousarray(g["sgu_bias"][Ls])
    poolw = np.ascontiguousarray(g["pool_w"][Ls])
    for core in range(8):
        b, half = core // 2, core % 2
        xs = np.zeros((TTOT, D), np.float32)
        if half == 0:
            xs[HALO:] = x[b, 0:TMAIN]
            flag = 0.0
        else:
            xs[:] = x[b, TMAIN - HALO:SEQ]
            flag = 1.0
        prm = np.zeros((NL, 128, 128), np.float32)
        for i, l in enumerate(Ls):
            prm[i, R_BIN:R_BIN + 24] = g["b_in"][l].reshape(24, 128)
            prm[i, R_CAW:R_CAW + 62] = g["conv_a_w"][l].reshape(62, 128)
            prm[i, R_CAB:R_CAB + 2] = g["conv_a_b"][l].reshape(2, 128)
            prm[i, R_NAG:R_NAG + 2] = g["norm_a_g"][l].reshape(2, 128)
            prm[i, R_NAB:R_NAB + 2] = g["norm_a_b"][l].reshape(2, 128)
            prm[i, R_CBW:R_CBW + 6] = g["conv_b_w"][l].reshape(6, 128)
            prm[i, R_PSC:R_PSC + 2] = g["pool_scale"][l].reshape(2, 128)
            prm[i, R_BOUT:R_BOUT + 8] = g["b_out"][l].reshape(8, 128)
            prm[i, R_LNG:R_LNG + 8] = g["ln_g"][l].reshape(8, 128)
            prm[i, R_LNB:R_LNB + 8] = g["ln_b"][l].reshape(8, 128)
            prm[i, R_PINV:R_PINV + 2] = pinv
            prm[i, R_FLAG] = flag
            prm[i, R_NEG1] = -1.0
        invc = np.zeros((128, 32), np.float32)
        for c in range(2):
            for h in range(2):
                wv = wins[2 * c + h]
                for t in range(16):
                    cnt = min(t + 1, wv) if half == 0 else wv
                    invc[h * 64:(h + 1) * 64, c * 16 + t] = 1.0 / cnt
        in_maps.append({"x": xs, "w_in": w_in, "w_out": w_out, "prm": prm, "dvb": dvb, "sguwT": sguwT,
                        "sgub": sgub, "pool_w": poolw, "ident": ident, "invc": invc})
    return in_maps


_NC_CACHE = {}


def kernel(**inputs):
    if "nc" not in _NC_CACHE:
        _NC_CACHE["nc"] = build(DEPTH)
    nc = _NC_CACHE["nc"]
    in_maps = host_prep(inputs, DEPTH)
    res = run_bass_kernel_spmd(nc, in_maps, core_ids=list(range(8)))
    out = np.zeros((BATCH, SEQ, D), np.float32)
    for core in range(8):
        b, half = core // 2, core % 2
        out[b, half * TMAIN:(half + 1) * TMAIN] = res.results[core]["y"]
    return out
```
